# Optimizing a Trainium2 kernel written in Bass

```python
import jax, jax.numpy as jnp
from jax import lax
import numpy as np

D_MODEL = 1024
BATCH = 8
SEQ = 2048
DEPTH = 4
DEC_BATCH = 128
DEC_SEQ = 8
PAST_LEN = 16384
PAGE_SIZE = 128

N_META = 16
N_MIXERS = 2
N_A = (DEPTH + 1) // 2
N_B = DEPTH // 2
D_CONV = D_MODEL
CONV_W = 31
POOL_WINDOWS = (2, 4, 8, 16)
N_GROUPS = len(POOL_WINDOWS)
GROUP_DIM = D_MODEL // N_GROUPS
POOL_MAX = max(POOL_WINDOWS)
D_FF = 2816
FFN_CONV_W = 3
EPS = 1e-6

kernel_name = "hybrid_conv_pool_decoder_step"


def rmsnorm(x, g):
    xf = x.astype(jnp.float32)
    y = xf * lax.rsqrt(jnp.mean(xf * xf, axis=-1, keepdims=True) + EPS)
    return (y * g.astype(jnp.float32)).astype(x.dtype)


def layernorm(x, g, b):
    xf = x.astype(jnp.float32)
    mu = jnp.mean(xf, axis=-1, keepdims=True)
    var = jnp.mean(jnp.square(xf - mu), axis=-1, keepdims=True)
    y = (xf - mu) * lax.rsqrt(var + EPS)
    return (y * g.astype(jnp.float32) + b.astype(jnp.float32)).astype(x.dtype)


def causal_dwconv(x, prefix, w, b):
    width = w.shape[0]
    xp = jnp.concatenate([prefix.astype(x.dtype), x], axis=1)
    out = lax.conv_general_dilated(
        xp, w[:, None, :].astype(x.dtype), window_strides=(1,), padding="VALID",
        dimension_numbers=("NWC", "WIO", "NWC"), feature_group_count=x.shape[-1])
    return out + b.astype(x.dtype), xp[:, xp.shape[1] - (width - 1):]


def conv_mixer(h, prefix, w_in, b_in, w_dw, b_dw, ln_g, ln_b, w_out, b_out):
    a = h @ w_in + b_in
    u, g = jnp.split(a, 2, axis=-1)
    v = u * jax.nn.sigmoid(g)
    c, new_prefix = causal_dwconv(v, prefix, w_dw, b_dw)
    c = jax.nn.silu(layernorm(c, ln_g, ln_b))
    return c @ w_out + b_out, new_prefix


def pool_mixer(h, prefix, start_pos, w_grp, scale):
    bsz, t_len, d = h.shape
    p = POOL_MAX - 1
    hp = jnp.concatenate([prefix.astype(h.dtype), h], axis=1)
    cs = jnp.concatenate([jnp.zeros((bsz, 1, d), jnp.float32),
                          jnp.cumsum(hp.astype(jnp.float32), axis=1)], axis=1)
    pos = start_pos + jnp.arange(t_len)
    parts = []
    for gi, w in enumerate(POOL_WINDOWS):
        sl = slice(gi * GROUP_DIM, (gi + 1) * GROUP_DIM)
        win = cs[:, p + 1:p + 1 + t_len, sl] - cs[:, p + 1 - w:p + 1 - w + t_len, sl]
        cnt = jnp.minimum(w, pos + 1).astype(jnp.float32)[None, :, None]
        parts.append(win / cnt)
    pooled = jnp.concatenate(parts, axis=-1)
    diff = (pooled - h.astype(jnp.float32)).astype(h.dtype).reshape(bsz, t_len, N_GROUPS, GROUP_DIM)
    y = jnp.einsum("btgc,gce->btge", diff, w_grp).reshape(bsz, t_len, d)
    return y * scale, hp[:, hp.shape[1] - p:]


def conv_ffn(h, prefix, w_up, w_dw, b_dw, w_down):
    u = h @ w_up
    c, new_prefix = causal_dwconv(u, prefix, w_dw, b_dw)
    gate, val = jnp.split(c, 2, axis=-1)
    return (jax.nn.silu(gate) * val) @ w_down, new_prefix


def trunk(x, st_conv, st_pool, st_ffn, start_pos, p):
    new_conv, new_pool, new_ffn = [], [], []
    for i in range(DEPTH):
        h = rmsnorm(x, p["mix_pre"][i])
        if i % N_MIXERS == 0:
            j = i // N_MIXERS
            m, ns = conv_mixer(h, st_conv[j], p["a_w_in"][j], p["a_b_in"][j], p["a_w_dw"][j],
                               p["a_b_dw"][j], p["a_ln_g"][j], p["a_ln_b"][j],
                               p["a_w_out"][j], p["a_b_out"][j])
            new_conv.append(ns)
        else:
            j = i // N_MIXERS
            m, ns = pool_mixer(h, st_pool[j], start_pos, p["b_w_grp"][j], p["b_scale"][j])
            new_pool.append(ns)
        x = x + rmsnorm(m, p["mix_post"][i])
        h = rmsnorm(x, p["ffn_pre"][i])
        f, ns = conv_ffn(h, st_ffn[i], p["f_w_up"][i], p["f_w_dw"][i], p["f_b_dw"][i], p["f_w_down"][i])
        new_ffn.append(ns)
        x = x + rmsnorm(f, p["ffn_post"][i])
    y = rmsnorm(x, p["final_norm"])
    return y, jnp.stack(new_conv), jnp.stack(new_pool), jnp.stack(new_ffn)


def setup_inputs(seed: int = 0) -> dict:
    key = jax.random.key(seed)
    ks = jax.random.split(key, 32)
    f32 = jnp.float32
    nrm = lambda k, shape, s=1.0: (jax.random.normal(k, shape, f32) * s)
    gain = lambda k, shape: 1.0 + 0.05 * jax.random.normal(k, shape, f32)
    return {
        "x_prompt": nrm(ks[0], (BATCH, SEQ, D_MODEL)),
        "x_sample": nrm(ks[1], (DEC_BATCH, DEC_SEQ, D_MODEL)),
        "state_conv": nrm(ks[2], (N_A, DEC_BATCH, CONV_W - 1, D_CONV), 0.5),
        "state_pool": nrm(ks[3], (N_B, DEC_BATCH, POOL_MAX - 1, D_MODEL)),
        "state_ffn": nrm(ks[4], (DEPTH, DEC_BATCH, FFN_CONV_W - 1, 2 * D_FF)),
        "meta_tokens": nrm(ks[5], (N_META, D_MODEL)),
        "mix_pre": gain(ks[6], (DEPTH, D_MODEL)),
        "mix_post": gain(ks[7], (DEPTH, D_MODEL)),
        "ffn_pre": gain(ks[8], (DEPTH, D_MODEL)),
        "ffn_post": gain(ks[9], (DEPTH, D_MODEL)),
        "final_norm": gain(ks[10], (D_MODEL,)),
        "a_w_in": nrm(ks[11], (N_A, D_MODEL, 2 * D_CONV), D_MODEL ** -0.5),
        "a_b_in": nrm(ks[12], (N_A, 2 * D_CONV), 0.02),
        "a_w_dw": nrm(ks[13], (N_A, CONV_W, D_CONV), CONV_W ** -0.5),
        "a_b_dw": nrm(ks[14], (N_A, D_CONV), 0.02),
        "a_ln_g": gain(ks[15], (N_A, D_CONV)),
        "a_ln_b": nrm(ks[16], (N_A, D_CONV), 0.02),
        "a_w_out": nrm(ks[17], (N_A, D_CONV, D_MODEL), D_CONV ** -0.5),
        "a_b_out": nrm(ks[18], (N_A, D_MODEL), 0.02),
        "b_w_grp": nrm(ks[19], (N_B, N_GROUPS, GROUP_DIM, GROUP_DIM), GROUP_DIM ** -0.5),
        "b_scale": gain(ks[20], (N_B, D_MODEL)),
        "f_w_up": nrm(ks[21], (DEPTH, D_MODEL, 2 * D_FF), D_MODEL ** -0.5),
        "f_w_dw": nrm(ks[22], (DEPTH, FFN_CONV_W, 2 * D_FF), FFN_CONV_W ** -0.5),
        "f_b_dw": nrm(ks[23], (DEPTH, 2 * D_FF), 0.02),
        "f_w_down": nrm(ks[24], (DEPTH, D_FF, D_MODEL), D_FF ** -0.5),
    }


def reference(x_prompt, x_sample, state_conv, state_pool, state_ffn, meta_tokens,
              mix_pre, mix_post, ffn_pre, ffn_post, final_norm,
              a_w_in, a_b_in, a_w_dw, a_b_dw, a_ln_g, a_ln_b, a_w_out, a_b_out,
              b_w_grp, b_scale, f_w_up, f_w_dw, f_b_dw, f_w_down):
    p = {"mix_pre": mix_pre, "mix_post": mix_post, "ffn_pre": ffn_pre, "ffn_post": ffn_post,
         "final_norm": final_norm, "a_w_in": a_w_in, "a_b_in": a_b_in, "a_w_dw": a_w_dw,
         "a_b_dw": a_b_dw, "a_ln_g": a_ln_g, "a_ln_b": a_ln_b, "a_w_out": a_w_out,
         "a_b_out": a_b_out, "b_w_grp": b_w_grp, "b_scale": b_scale, "f_w_up": f_w_up,
         "f_w_dw": f_w_dw, "f_b_dw": f_b_dw, "f_w_down": f_w_down}
    dt = x_prompt.dtype
    bsz = x_prompt.shape[0]
    meta = jnp.broadcast_to(meta_tokens.astype(dt)[None], (bsz, N_META, D_MODEL))
    xp = jnp.concatenate([meta, x_prompt], axis=1)
    z_conv = jnp.zeros((N_A, bsz, CONV_W - 1, D_CONV), dt)
    z_pool = jnp.zeros((N_B, bsz, POOL_MAX - 1, D_MODEL), dt)
    z_ffn = jnp.zeros((DEPTH, bsz, FFN_CONV_W - 1, 2 * D_FF), dt)
    yp, nc_p, np_p, nf_p = trunk(xp, z_conv, z_pool, z_ffn, 0, p)
    y_prompt = yp[:, N_META:]
    y_sample, nc_s, np_s, nf_s = trunk(x_sample, state_conv, state_pool, state_ffn, PAST_LEN, p)
    return (y_prompt, y_sample, nc_p, np_p, nf_p, nc_s, np_s, nf_s)
```

```python
import contextlib
import numpy as np
import concourse.bass as bass
import concourse.mybir as mybir
from concourse.bass_utils import run_bass_kernel_spmd

F32 = mybir.dt.float32
BF16 = mybir.dt.bfloat16
AF = mybir.ActivationFunctionType
ALU = mybir.AluOpType

D = 1024
KC = 8
DFF = 2816
JC = 22
UC = 44
NL = 4
EPS = 1e-6
NMETA = 16
SEQ = 2048
LP = 688
NT = 344
NPSEG = 3
PR = 48
R_MIXPRE, R_MIXPOST, R_FFNPRE, R_FFNPOST, R_BU, R_BG, R_BDW, R_LNG, R_LNB, R_BOUT, R_SCALE, R_FINAL, R_WDW = \
    0, 1, 2, 3, 4, 5, 6, 7, 8, 9, 10, 11, 12
POOLW = (2, 4, 8, 16)
SEM_EPOCH = 30000
DMA_SLOTS = 8


class _Eng:
    def __init__(self, h, name):
        self.h = h
        self.name = name
        self.sem = None
        self.n = 0
        self.epoch = -1
        self.seen = {}


class Sched:
    def __init__(self, nc, es):
        self.nc = nc
        self.es = es
        self.eng = {}
        for name, h in (("pe", nc.tensor), ("act", nc.scalar), ("dve", nc.vector),
                        ("pool", nc.gpsimd), ("sp", nc.sync)):
            self.eng[name] = _Eng(h, name)
        self.lastw = {}
        self.readers = {}
        self.dq = {}
        self.nsem = 0

    def _newsem(self, nm):
        self.nsem += 1
        return self.es.enter_context(self.nc.semaphore(f"{nm}_{self.nsem}"))

    def add_dma_queue(self, qn, en):
        self.dq[qn] = dict(en=en, sems=[self._newsem(qn) for _ in range(DMA_SLOTS)], i=0)

    def _collect(self, reads, writes):
        d = {}

        def add(ev):
            if ev is None:
                return
            o = d.get(ev[0])
            if o is None or o[2] < ev[2]:
                d[ev[0]] = ev
        for k in reads:
            add(self.lastw.get(k))
        for k in writes:
            add(self.lastw.get(k))
            rd = self.readers.get(k)
            if rd:
                for ev in rd.values():
                    add(ev)
        return d

    def _wait(self, en, d):
        e = self.eng[en]
        for name, sem, val in d.values():
            if en == "pe" and name.startswith("pe#"):
                continue
            if e.seen.get(name, 0) >= val:
                continue
            e.h.wait_ge(sem, val)
            e.seen[name] = val

    def _commit(self, ev, reads, writes):
        for k in writes:
            self.lastw[k] = ev
            self.readers[k] = {}
        for k in reads:
            self.readers.setdefault(k, {})[ev[0]] = ev

    def op(self, en, fn, reads=(), writes=()):
        e = self.eng[en]
        self._wait(en, self._collect(reads, writes))
        if e.sem is None or e.n >= SEM_EPOCH:
            e.epoch += 1
            e.n = 0
            e.sem = self._newsem(en)
        ins = fn(e.h)
        e.n += 1
        ins.then_inc(e.sem, 1)
        e.last_ev = (f"{en}#{e.epoch}", e.sem, e.n)
        self._commit(e.last_ev, reads, writes)

    def barrier(self):
        evs = {}
        for en, e in self.eng.items():
            ev = getattr(e, "last_ev", None)
            if ev is not None:
                evs[ev[0]] = ev
        for qn, q in self.dq.items():
            for slot in range(DMA_SLOTS):
                cnt = (q["i"] - slot + DMA_SLOTS - 1) // DMA_SLOTS
                if cnt > 0:
                    evs[f"{qn}{slot}"] = (f"{qn}{slot}", q["sems"][slot], 16 * cnt)
        for en in self.eng:
            e = self.eng[en]
            for name, sem, val in evs.values():
                if e.seen.get(name, 0) >= val:
                    continue
                e.h.wait_ge(sem, val)
                e.seen[name] = val
        self.lastw.clear()
        self.readers.clear()

    def dma(self, qn, out, in_, reads=(), writes=()):
        q = self.dq[qn]
        en = q["en"]
        e = self.eng[en]
        slot = q["i"] % DMA_SLOTS
        rnd = q["i"] // DMA_SLOTS
        q["i"] += 1
        name = f"{qn}{slot}"
        sem = q["sems"][slot]
        d = self._collect(reads, writes)
        if rnd > 0:
            d[name] = (name, sem, 16 * rnd)
        self._wait(en, d)
        e.h.dma_start(out=out, in_=in_).then_inc(sem, 16)
        self._commit((name, sem, 16 * (rnd + 1)), reads, writes)

    def finish(self):
        for qn, q in self.dq.items():
            e = self.eng[q["en"]]
            for slot in range(DMA_SLOTS):
                cnt = (q["i"] - slot + DMA_SLOTS - 1) // DMA_SLOTS
                if cnt > 0:
                    e.h.wait_ge(q["sems"][slot], 16 * cnt)


class Seg:
    def __init__(self, kind, idx):
        self.kind = kind
        self.idx = idx
        if kind == "P":
            self.ns, self.L = 1, LP
            self.tiles = [(0, NT), (NT, NT)]
        else:
            self.ns, self.L = 16, 8
            self.tiles = [(0, 8)]
        self.first = (kind == "P" and idx == 0)
        self.last = (kind == "S") or (idx == NPSEG - 1)


class Buf:
    def __init__(self, t, name, seg, halo, base=0):
        self.t = t
        self.rl = t.shape[1]
        self.name = name
        self.ns = seg.ns
        self.halo = halo
        self.SS = halo + seg.L
        self.CS = seg.ns * self.SS
        self.base = base
        self.L = seg.L
        self.ntl = NT if seg.kind == "P" else seg.L

    def ap(self, c, t0, nt, nch=1):
        off = self.base + c * self.CS + self.halo + t0
        dims = [[self.rl, 128]]
        if nch > 1:
            dims.append([self.CS, nch])
        if self.ns > 1:
            dims.append([self.SS, self.ns])
        dims.append([1, nt])
        return bass.AP(self.t, off, dims)

    def keys(self, c, t0, nt, nch=1):
        lo = max(t0, 0) // self.ntl
        hi = max(t0 + nt - 1, 0) // self.ntl
        ks = []
        for cc in range(c, c + nch):
            for ti in range(lo, hi + 1):
                ks.append((self.name, cc, ti))
            if t0 < 0:
                ks.append((self.name, cc, "h"))
        return ks


def build(nseg_list=("P0", "P1", "P2", "S")):
    nc = bass.Bass("TRN2", target_bir_lowering=False)

    def din(name, shape):
        return nc.dram_tensor(name, list(shape), F32, kind="ExternalInput").ap()

    def dout(name, shape):
        return nc.dram_tensor(name, list(shape), F32, kind="ExternalOutput").ap()

    xp = din("xp", (SEQ, D))
    xs = din("xs", (128, D))
    sconv = din("sconv", (2, 480, D))
    spool = din("spool", (2, 240, D))
    sffn = din("sffn", (NL, 32, 2 * DFF))
    meta = din("meta", (NMETA, D))
    prm = din("prm", (NL, PR, D))
    fprm = din("fprm", (NL, 4, 2 * DFF))
    a_w_in = din("a_w_in", (2, D, 2 * D))
    a_w_out = din("a_w_out", (2, D, D))
    b_w_grp = din("b_w_grp", (2, 4, 256, 256))
    f_w_up = din("f_w_up", (NL, D, 2 * DFF))
    f_w_down = din("f_w_down", (NL, DFF, D))
    yp = dout("yp", (SEQ, D))
    ys = dout("ys", (128, D))
    ncp = dout("ncp", (2, 30, D))
    npp = dout("npp", (2, 15, D))
    nfp = dout("nfp", (NL, 2, 2 * DFF))
    ncs = dout("ncs", (2, 480, D))
    nps = dout("nps", (2, 240, D))
    nfs = dout("nfs", (NL, 32, 2 * DFF))

    with contextlib.ExitStack() as es:
        S = Sched(nc, es)
        S.add_dma_queue("qi", "sp")
        S.add_dma_queue("qo", "sp")
        S.add_dma_queue("qw", "pool")
        S.add_dma_queue("qs", "sp")
        WSCR = nc.dram_tensor("wscr", [152, 128, JC * 128], BF16).ap()

        def sb(name, cols, dt=F32):
            return es.enter_context(nc.sbuf_tensor(name, [128, cols], dt))

        XB = sb("XB", KC * LP)
        R1 = sb("R1", KC * (LP + 2) // 2)
        R1b = R1.bitcast(BF16)
        R2 = sb("R2", JC * LP // 2)
        R2b = R2.bitcast(BF16)
        R3 = sb("R3", KC * LP)
        NW8, NW22 = 5, 4
        W8 = [sb(f"W8_{i}", KC * 256, BF16) for i in range(NW8)]
        W22 = [sb(f"W22_{i}", JC * 128, BF16) for i in range(NW22)]
        STI = [sb(f"STI{i}", 1024) for i in range(3)]
        STO = [sb(f"STO{i}", 1024) for i in range(2)]
        PRM = sb("PRM", NL * KC * PR)
        FPRM = sb("FPRM", NL * UC * 4)
        IDN = sb("IDN", 128)
        ONES = sb("ONES", 128, BF16)
        NSCR = 2
        SQ = [sb(f"SQ{i}", NT, BF16) for i in range(2)]
        CB16 = [sb(f"CB16_{i}", NT, BF16) for i in range(NSCR)]
        RS = [sb(f"RS{i}", NT) for i in range(NSCR)]
        TT = [sb(f"TT{i}", NT) for i in range(3)]
        GS = [sb(f"GS{i}", NT) for i in range(3)]
        VS = [sb(f"VS{i}", NT) for i in range(3)]
        CST = sb("CST", 2 * KC * 30, BF16)
        VFT = sb("VFT", KC * 128)
        DG = [sb(f"DG{i}", 31 * 128, BF16) for i in range(2)]
        PST = sb("PST", 2 * KC * 15)
        FH = sb("FH", NL * KC * 2, BF16)
        UST = sb("UST", UC * 32)
        UO = sb("UO", UC * 32)
        CT = [sb(f"CT{i}", 128) for i in range(4)]
        IC = sb("IC", 4 * 16)
        PS = [es.enter_context(nc.psum_tensor(f"PS{i}", [128, 512], F32)) for i in range(8)]

        st = dict(ps=0, w8=0, w22=0, sti=0, sto=0, scr={})

        def V(t, off, dims, parts=128):
            return bass.AP(t, off, [[t.shape[1], parts]] + [list(x) for x in dims])

        def next_ps():
            i = st["ps"] % 8
            st["ps"] += 1
            return PS[i], ("ps", i)

        def rot(lst, nm):
            i = st["scr"].get(nm, 0)
            st["scr"][nm] = i + 1
            j = i % len(lst)
            return lst[j], (nm, j)

        S.op("pool", lambda h: h.memset(IDN[:], 0.0), writes=[("IDN",)])
        S.op("pool", lambda h: h.affine_select(out=IDN[:], in_=IDN[:], pattern=[[-1, 128]],
                                               compare_op=ALU.not_equal, fill=1.0, base=0,
                                               channel_multiplier=1), reads=[("IDN",)], writes=[("IDN",)])
        S.op("pool", lambda h: h.memset(ONES[:], 1.0), writes=[("ONES",)])
        for g, w in enumerate(POOLW):
            S.op("pool", lambda h: h.memset(IC[:, 16 * g:16 * g + 16], 1.0 / w), writes=[("IC",)])
            for t in range(w - 1):
                S.op("pool", lambda h: h.memset(IC[:, 16 * g + t:16 * g + t + 1], 1.0 / (t + 1)),
                     writes=[("IC",)])

        def load_T(src_rows, R, ncol_chunks, dst_fn, dst_keys_fn):
            stg, sk = rot(STI, "sti")
            W = ncol_chunks * 128
            for (ap, r0, nr) in src_rows:
                S.dma("qi", V(stg, r0 * 1024, [[1, W]], parts=nr), ap, writes=[sk])
            per = max(1, min(ncol_chunks, 512 // R))
            c0 = 0
            while c0 < ncol_chunks:
                n = min(per, ncol_chunks - c0)
                ps, pk = next_ps()

                def fn(h, c0=c0, n=n, ps=ps):
                    for i in range(n):
                        ins = h.transpose(V(ps, i * R, [[1, R]]),
                                          V(stg, (c0 + i) * 128, [[1, 128]], parts=R),
                                          V(IDN, 0, [[1, R]], parts=R))
                    return ins
                S.op("pe", fn, reads=[sk, ("IDN",)], writes=[pk])
                dst = dst_fn(c0, n)
                shp = list(dst.shape)[1:]
                dims, stp = [], 1
                for cnt in reversed(shp):
                    dims.insert(0, [stp, cnt])
                    stp *= cnt
                src = V(ps, 0, dims)
                S.op("act", lambda h, dst=dst, src=src: h.activation(out=dst, in_=src, func=AF.Copy),
                     reads=[pk], writes=dst_keys_fn(c0, n))
                c0 += n

        def store_T(src_fn, src_keys_fn, R, ncol_chunks, dst_pieces):
            stg, sk = rot(STO, "sto")
            c0 = 0
            while c0 < ncol_chunks:
                n = min(4, ncol_chunks - c0)
                ps, pk = next_ps()

                def fn(h, c0=c0, n=n, ps=ps):
                    for i in range(n):
                        ins = h.transpose(V(ps, i * 128, [[1, 128]], parts=R), src_fn(c0 + i), IDN[:])
                    return ins
                rk = []
                for i in range(n):
                    rk += src_keys_fn(c0 + i)
                S.op("pe", fn, reads=rk + [("IDN",)], writes=[pk])
                S.op("act", lambda h, c0=c0, n=n, ps=ps: h.activation(
                    out=V(stg, c0 * 128, [[1, n * 128]], parts=R),
                    in_=V(ps, 0, [[1, n * 128]], parts=R), func=AF.Copy),
                    reads=[pk], writes=[sk])
                c0 += n
            W = ncol_chunks * 128
            for (ap, r0, nr) in dst_pieces:
                S.dma("qo", ap, V(stg, r0 * 1024, [[1, W]], parts=nr), reads=[sk])

        def store_state_S(B, Rr, src_d, dst_d, contig=None, ckeys=None):
            keep = Rr - 8
            S.dma("qo", dst_d.rearrange("(s r) d -> s r d", r=Rr)[:, 0:keep, :],
                  src_d.rearrange("(s r) d -> s r d", r=Rr)[:, 8:Rr, :])
            stg, sk = rot(STO, "sto")
            for c0 in range(0, KC, 4):
                ps, pk = next_ps()
                cts = []
                for i in range(4):
                    if contig is not None:
                        cts.append((contig(c0 + i), ckeys(c0 + i)))
                        continue
                    ct, ctk = rot(CT, "CT")
                    S.op("act", lambda h, ct=ct, i=i: h.activation(out=V(ct, 0, [[8, 16], [1, 8]]),
                                                                   in_=B.ap(c0 + i, 0, 8), func=AF.Copy),
                         reads=B.keys(c0 + i, 0, 8), writes=[ctk])
                    cts.append((ct[:], [ctk]))

                def fn(h, ps=ps, cts=cts):
                    for i, (cta, _) in enumerate(cts):
                        ins = h.transpose(V(ps, i * 128, [[1, 128]]), cta, IDN[:])
                    return ins
                S.op("pe", fn, reads=sum([k for _, k in cts], []) + [("IDN",)], writes=[pk])
                S.op("act", lambda h, ps=ps, c0=c0: h.activation(out=V(stg, c0 * 128, [[1, 512]]),
                                                                 in_=V(ps, 0, [[1, 512]]), func=AF.Copy),
                     reads=[pk], writes=[sk])
            for s_ in range(16):
                S.dma("qo", dst_d[s_ * Rr + keep:(s_ + 1) * Rr, :], V(stg, s_ * 8 * 1024, [[1, 1024]], parts=8),
                      reads=[sk])

        def load_params(l):
            load_T([(prm[l], 0, PR)], PR, KC,
                   lambda c0, n: V(PRM, (l * KC + c0) * PR, [[PR, n], [1, PR]]),
                   lambda c0, n: [("PRM", l)])
            for q in range(6):
                c0 = q * 8
                n = min(8, UC - c0)
                load_T([(fprm[l][:, c0 * 128:(c0 + n) * 128], 0, 4)], 4, n,
                       lambda cc, nn, c0=c0: V(FPRM, (l * UC + c0 + cc) * 4, [[4, nn], [1, 4]]),
                       lambda cc, nn: [("FPRM", l)])


        def P(l, c, r):
            return V(PRM, (l * KC + c) * PR + r, [[1, 1]])

        def FP(l, c, r):
            return V(FPRM, (l * UC + c) * 4 + r, [[1, 1]])

        scr_map = {}

        def _load_w(t, key, n, cast_out, cast_in, uid):
            if uid not in scr_map:
                u = scr_map[uid] = len(scr_map)
                S.dma("qw", cast_out, cast_in, writes=[key])
                S.dma("qs", WSCR[u][:, 0:n], V(t, 0, [[1, n]]), reads=[key], writes=[("scr", u)])
            else:
                u = scr_map[uid]
                S.dma("qs", V(t, 0, [[1, n]]), WSCR[u][:, 0:n], reads=[("scr", u)], writes=[key])

        def load_w8(src_ap, kch, uid):
            i = st["w8"] % NW8
            st["w8"] += 1
            t = W8[i]
            _load_w(t, ("w8", i), kch * 256, V(t, 0, [[256, kch], [1, 256]]),
                    src_ap.rearrange("(k p) n -> p k n", p=128), uid)
            return t, ("w8", i)

        def load_w22(src_ap, uid):
            i = st["w22"] % NW22
            st["w22"] += 1
            t = W22[i]
            _load_w(t, ("w22", i), JC * 128, V(t, 0, [[128, JC], [1, 128]]),
                    src_ap.rearrange("(k p) n -> p k n", p=128), uid)
            return t, ("w22", i)

        class WStream:
            def __init__(self, units, ahead):
                self.units = units
                self.loaded = []
                self.nxt = 0
                self.ahead = ahead

            def _issue(self):
                kind, src, kch, uid = self.units[self.nxt]
                self.loaded.append(load_w8(src, kch, uid) if kind == 8 else load_w22(src, uid))
                self.nxt += 1

            def get(self, i):
                while self.nxt < len(self.units) and self.nxt <= i + self.ahead:
                    self._issue()
                return self.loaded[i]

        def tile_cols(seg, nt):
            return seg.ns * nt

        def flat(t, seg, nt, off=0):
            if seg.ns > 1:
                return V(t, off, [[nt, seg.ns], [1, nt]])
            return V(t, off, [[1, nt]])

        def sumsq_rstd(seg, nt, sq_src_fn, sq_keys_fn, sq_scale=None, sq_bias=None):
            ps, pk = next_ps()
            n = tile_cols(seg, nt)
            for c in range(KC):
                sq, sqk = rot(SQ, "SQ")
                kw = {}
                if sq_scale is not None:
                    kw["scale"] = sq_scale(c)
                if sq_bias is not None:
                    kw["bias"] = sq_bias(c)
                if False:
                    S.op("pool", lambda h, c=c, sq=sq: h.tensor_tensor(
                        out=flat(sq, seg, nt), in0=sq_src_fn(c), in1=sq_src_fn(c), op=ALU.mult),
                        reads=sq_keys_fn(c), writes=[sqk])
                else:
                    S.op("act", lambda h, c=c, sq=sq, kw=kw: h.activation(
                        out=flat(sq, seg, nt), in_=sq_src_fn(c), func=AF.Square, **kw),
                        reads=sq_keys_fn(c), writes=[sqk])
                S.op("pe", lambda h, c=c, sq=sq, ps=ps: h.matmul(
                    V(ps, 0, [[1, n]]), ONES[:], V(sq, 0, [[1, n]]), start=(c == 0), stop=(c == KC - 1)),
                    reads=[sqk, ("ONES",)], writes=[pk])
            rs, rsk = rot(RS, "RS")
            S.op("act", lambda h: h.activation(out=V(rs, 0, [[1, n]]), in_=V(ps, 0, [[1, n]]), func=AF.Ln,
                                               bias=EPS, scale=1.0 / D), reads=[pk], writes=[rsk])
            S.op("act", lambda h: h.activation(out=V(rs, 0, [[1, n]]), in_=V(rs, 0, [[1, n]]), func=AF.Exp,
                                               scale=-0.5), reads=[rsk], writes=[rsk])
            return rs, rsk

        def prenorm(seg, X, dst, l, grow):
            for (t0, nt) in seg.tiles:
                rs, rsk = sumsq_rstd(seg, nt, lambda c: X.ap(c, t0, nt), lambda c: X.keys(c, t0, nt))
                for c in range(KC):
                    S.op("dve", lambda h, c=c: h.scalar_tensor_tensor(
                        out=dst.ap(c, t0, nt), in0=X.ap(c, t0, nt), scalar=P(l, c, grow),
                        in1=flat(rs, seg, nt), op0=ALU.mult, op1=ALU.mult),
                        reads=X.keys(c, t0, nt) + [rsk, ("PRM", l)], writes=dst.keys(c, t0, nt))

        def postnorm_add(seg, X, M, l, grow, t0, nt, rs, rsk):
            for c in range(KC):
                tt, ttk = rot(TT, "TT")
                S.op("pool" if c >= 4 else "dve", lambda h, c=c, tt=tt: h.tensor_tensor(
                    out=flat(tt, seg, nt), in0=M.ap(c, t0, nt), in1=flat(rs, seg, nt), op=ALU.mult),
                    reads=M.keys(c, t0, nt) + [rsk], writes=[ttk])
                S.op("dve", lambda h, c=c, tt=tt: h.scalar_tensor_tensor(
                    out=X.ap(c, t0, nt), in0=flat(tt, seg, nt), scalar=P(l, c, grow), in1=X.ap(c, t0, nt),
                    op0=ALU.mult, op1=ALU.add),
                    reads=X.keys(c, t0, nt) + [ttk, ("PRM", l)], writes=X.keys(c, t0, nt))

        segs = []
        for nm in nseg_list:
            segs.append(Seg("S", 0) if nm == "S" else Seg("P", int(nm[1])))

        for si, seg in enumerate(segs):
            if si > 0 and segs[si - 1].kind != seg.kind:
                S.barrier()
            ns, L = seg.ns, seg.L
            isS = seg.kind == "S"
            ntl_ = len(seg.tiles)
            R2ALL = [("A", q, ti) for q in range(JC) for ti in range(ntl_)]
            for nm_ in ("V", "Hf"):
                R2ALL += [(nm_, c, ti) for c in range(KC) for ti in list(range(ntl_)) + ["h"]]
            R2ALL += [(nm_, 0, ti) for nm_ in ("PWa0", "PWb0") for ti in list(range(ntl_)) + ["h"]]
            PW1ALL = [(nm_, 0, ti) for nm_ in ("PWa1", "PWb1") for ti in list(range(ntl_)) + ["h"]]
            X = Buf(XB, "X", seg, 0)
            Hh = Buf(R1b, "H", seg, 2)
            A = Buf(R2b, "A", seg, 0)
            Vb = Buf(R2b, "V", seg, 30)
            Hf = Buf(R2, "Hf", seg, 15)
            Cb = Buf(R3, "C", seg, 0)

            def load_halo_S(l_):
                j_ = l_ // 2
                if l_ % 2 == 0:
                    for q in range(4):
                        load_T([(sconv[j_][q * 120:(q + 1) * 120, :], 0, 120)], 120, KC,
                               lambda c0, n, q=q: V(R2b, c0 * Vb.CS + 4 * q * Vb.SS,
                                                    [[Vb.CS, n], [Vb.SS, 4], [1, 30]]),
                               lambda c0, n: R2ALL)
                else:
                    for q in range(2):
                        load_T([(spool[j_][q * 120:(q + 1) * 120, :], 0, 120)], 120, KC,
                               lambda c0, n, q=q: V(R2, c0 * Hf.CS + 8 * q * Hf.SS,
                                                    [[Hf.CS, n], [Hf.SS, 8], [1, 15]]),
                               lambda c0, n: R2ALL)

            if isS:
                load_halo_S(0)
            if isS:
                load_T([(xs[:, :], 0, 128)], 128, KC,
                       lambda c0, n: V(XB, c0 * X.CS, [[X.CS, n], [1, 128]]),
                       lambda c0, n: [("X", c, 0) for c in range(c0, c0 + n)])
            else:
                p0 = seg.idx * LP
                r = 0
                while r < LP:
                    nr = min(128, LP - r)
                    pieces = []
                    a = p0 + r
                    if a < NMETA:
                        nm_ = min(NMETA - a, nr)
                        pieces.append((meta[a:a + nm_, :], 0, nm_))
                        if nr > nm_:
                            pieces.append((xp[0:nr - nm_, :], nm_, nr - nm_))
                    else:
                        pieces.append((xp[a - NMETA:a - NMETA + nr, :], 0, nr))
                    load_T(pieces, nr, KC,
                           lambda c0, n, r=r, nr=nr: V(XB, c0 * X.CS + r, [[X.CS, n], [1, nr]]),
                           lambda c0, n, r=r, nr=nr: sum([X.keys(c, r, nr) for c in range(c0, c0 + n)], []))
                    r += nr

            if si == 0:
                load_params(0)
            for l in range(NL):
                j = l // 2
                if isS:
                    for q in range(6):
                        c0 = q * 8
                        n = min(8, UC - c0)
                        load_T([(sffn[l][:, c0 * 128:(c0 + n) * 128], 0, 32)], 32, n,
                               lambda cc, nn, c0=c0: V(UST, (c0 + cc) * 32, [[32, nn], [1, 32]]),
                               lambda cc, nn: [("UST",)])
                if l % 2 == 0:
                    units = []
                    for b in range(4):
                        units.append((8, a_w_in[j][:, 256 * b:256 * b + 256], KC, ("win", j, b, 0)))
                        units.append((8, a_w_in[j][:, D + 256 * b:D + 256 * b + 256], KC, ("win", j, b, 1)))
                    for b in range(4):
                        units.append((8, a_w_out[j][:, 256 * b:256 * b + 256], KC, ("wout", j, b)))
                    ws = WStream(units, 3)
                    ws.get(0)
                    if isS:
                        pass
                    elif seg.first:
                        S.op("pool", lambda h: h.memset(V(R2b, 0, [[Vb.CS, KC], [1, 30]]), 0.0),
                             writes=R2ALL)
                    else:
                        S.op("act", lambda h: h.activation(out=V(R2b, 0, [[Vb.CS, KC], [1, 30]]),
                                                           in_=V(CST, j * KC * 30, [[30, KC], [1, 30]]),
                                                           func=AF.Copy),
                             reads=[("CST", j)], writes=R2ALL)
                    prenorm(seg, X, Hh, l, R_MIXPRE)
                    if si == 0 and l + 1 < NL:
                        load_params(l + 1)
                    for b in range(4):
                        wu, wuk = ws.get(2 * b)
                        wg, wgk = ws.get(2 * b + 1)
                        for (t0, nt) in seg.tiles:
                            n = ns * nt
                            for cc in range(2):
                                c = 2 * b + cc
                                pu, puk = next_ps()
                                pg, pgk = next_ps()
                                hk = sum([Hh.keys(k, t0, nt) for k in range(KC)], [])

                                def mmf(h, w, ps, cc=cc, t0=t0, nt=nt):
                                    for k in range(KC):
                                        ins = h.matmul(flat(ps, seg, nt), V(w, k * 256 + cc * 128, [[1, 128]]),
                                                       Hh.ap(k, t0, nt), start=(k == 0), stop=(k == KC - 1))
                                    return ins
                                S.op("pe", lambda h: mmf(h, wu, pu), reads=hk + [wuk], writes=[puk])
                                S.op("pe", lambda h: mmf(h, wg, pg), reads=hk + [wgk], writes=[pgk])
                                sg, sgk = rot(GS, "GS")
                                S.op("act", lambda h: h.activation(out=flat(sg, seg, nt), in_=flat(pg, seg, nt),
                                                                   func=AF.Sigmoid, bias=P(l, c, R_BG), scale=1.0),
                                     reads=[pgk, ("PRM", l)], writes=[sgk])
                                S.op("dve", lambda h: h.scalar_tensor_tensor(
                                    out=Vb.ap(c, t0, nt), in0=flat(pu, seg, nt), scalar=P(l, c, R_BU),
                                    in1=flat(sg, seg, nt), op0=ALU.add, op1=ALU.mult),
                                    reads=[puk, sgk, ("PRM", l)], writes=Vb.keys(c, t0, nt))
                                if isS:
                                    S.op("dve", lambda h: h.scalar_tensor_tensor(
                                        out=V(VFT, c * 128, [[8, 16], [1, 8]]), in0=flat(pu, seg, nt),
                                        scalar=P(l, c, R_BU), in1=flat(sg, seg, nt), op0=ALU.add, op1=ALU.mult),
                                        reads=[puk, sgk, ("PRM", l)], writes=[("VF", c)])
                                elif seg.last and t0 + nt == L:
                                    S.op("dve", lambda h: h.scalar_tensor_tensor(
                                        out=V(VFT, c * 128, [[1, 30]]), in0=V(pu, nt - 30, [[1, 30]]),
                                        scalar=P(l, c, R_BU), in1=V(sg, nt - 30, [[1, 30]]), op0=ALU.add, op1=ALU.mult),
                                        reads=[puk, sgk, ("PRM", l)], writes=[("VF", c)])
                    if isS:
                        store_state_S(None, 30, sconv[j], ncs[j], contig=lambda c: V(VFT, c * 128, [[1, 128]]),
                                      ckeys=lambda c: [("VF", c)])
                    elif seg.last:
                        store_T(lambda c: V(VFT, c * 128, [[1, 30]]), lambda c: [("VF", c)], 30, KC,
                                [(ncp[j], 0, 30)])
                    else:
                        S.op("act", lambda h: h.activation(out=V(CST, j * KC * 30, [[30, KC], [1, 30]]),
                                                           in_=V(R2b, Vb.halo + L - 30, [[Vb.CS, KC], [1, 30]]),
                                                           func=AF.Copy),
                             reads=sum([Vb.keys(c, L - 30, 30) for c in range(KC)], []), writes=[("CST", j)])
                    for c in range(KC):
                        dg, dgk = rot(DG, "DG")
                        S.op("dve" if c % 2 == 0 else "pool", lambda h, dg=dg, c=c: h.tensor_tensor(
                            out=V(dg, 0, [[128, 31], [1, 128]]), in0=V(IDN, 0, [[0, 31], [1, 128]]),
                            in1=V(PRM, (l * KC + c) * PR + R_WDW, [[1, 31], [0, 128]]), op=ALU.mult),
                            reads=[("IDN",), ("PRM", l)], writes=[dgk])
                        for (t0, nt) in seg.tiles:
                            pc, pck = next_ps()

                            def mmc(h, dg=dg, pc=pc, c=c, t0=t0, nt=nt):
                                for k in range(31):
                                    ins = h.matmul(flat(pc, seg, nt), V(dg, k * 128, [[1, 128]]),
                                                   Vb.ap(c, t0 - 30 + k, nt), start=(k == 0), stop=(k == 30))
                                return ins
                            S.op("pe", mmc, reads=Vb.keys(c, t0 - 30, nt + 30) + [dgk], writes=[pck])
                            S.op("act", lambda h, pc=pc, c=c, t0=t0, nt=nt: h.activation(
                                out=Cb.ap(c, t0, nt), in_=flat(pc, seg, nt), func=AF.Identity,
                                bias=P(l, c, R_BDW), scale=1.0),
                                reads=[pck, ("PRM", l)], writes=Cb.keys(c, t0, nt))
                    ln_stats = []
                    for (t0, nt) in seg.tiles:
                        n = ns * nt
                        p1, p1k = next_ps()
                        p2, p2k = next_ps()
                        for c in range(KC):
                            cb, cbk = rot(CB16, "CB16")
                            sq, sqk = rot(SQ, "SQ")
                            S.op("dve", lambda h, c=c, cb=cb: h.tensor_copy(out=flat(cb, seg, nt), in_=Cb.ap(c, t0, nt)),
                                 reads=Cb.keys(c, t0, nt), writes=[cbk])
                            S.op("act", lambda h, c=c, sq=sq: h.activation(out=flat(sq, seg, nt), in_=Cb.ap(c, t0, nt),
                                                                           func=AF.Square),
                                 reads=Cb.keys(c, t0, nt), writes=[sqk])
                            S.op("pe", lambda h, c=c, cb=cb: h.matmul(V(p1, 0, [[1, n]]), ONES[:], V(cb, 0, [[1, n]]),
                                                                      start=(c == 0), stop=(c == KC - 1)),
                                 reads=[cbk, ("ONES",)], writes=[p1k])
                            S.op("pe", lambda h, c=c, sq=sq: h.matmul(V(p2, 0, [[1, n]]), ONES[:], V(sq, 0, [[1, n]]),
                                                                      start=(c == 0), stop=(c == KC - 1)),
                                 reads=[sqk, ("ONES",)], writes=[p2k])
                        mu, muk = rot(VS, "VS")
                        rs, rsk = rot(RS, "RS")
                        tt, ttk = rot(TT, "TT")
                        S.op("dve", lambda h: h.tensor_scalar(out=V(mu, 0, [[1, n]]), in0=V(p1, 0, [[1, n]]),
                                                              scalar1=1.0 / D, scalar2=None, op0=ALU.mult),
                             reads=[p1k], writes=[muk])
                        S.op("dve", lambda h: h.tensor_tensor(out=V(tt, 0, [[1, n]]), in0=V(mu, 0, [[1, n]]),
                                                              in1=V(mu, 0, [[1, n]]), op=ALU.mult),
                             reads=[muk], writes=[ttk])
                        S.op("dve", lambda h: h.scalar_tensor_tensor(
                            out=V(rs, 0, [[1, n]]), in0=V(p2, 0, [[1, n]]), scalar=1.0 / D, in1=V(tt, 0, [[1, n]]),
                            op0=ALU.mult, op1=ALU.subtract), reads=[p2k, ttk], writes=[rsk])
                        S.op("act", lambda h: h.activation(out=V(rs, 0, [[1, n]]), in_=V(rs, 0, [[1, n]]),
                                                           func=AF.Ln, bias=EPS, scale=1.0),
                             reads=[rsk], writes=[rsk])
                        S.op("act", lambda h: h.activation(out=V(rs, 0, [[1, n]]), in_=V(rs, 0, [[1, n]]),
                                                           func=AF.Exp, scale=-0.5),
                             reads=[rsk], writes=[rsk])
                        ln_stats.append((mu, muk, rs, rsk))
                    for (t0, nt), (mu, muk, rs, rsk) in zip(seg.tiles, ln_stats):
                        for c in range(KC):
                            S.op("pool" if c >= 4 else "dve", lambda h, c=c: h.tensor_tensor(out=Cb.ap(c, t0, nt), in0=Cb.ap(c, t0, nt),
                                                                        in1=flat(mu, seg, nt), op=ALU.subtract),
                                 reads=Cb.keys(c, t0, nt) + [muk], writes=Cb.keys(c, t0, nt))
                            S.op("dve", lambda h, c=c: h.tensor_tensor(out=Cb.ap(c, t0, nt), in0=Cb.ap(c, t0, nt),
                                                                       in1=flat(rs, seg, nt), op=ALU.mult),
                                 reads=Cb.keys(c, t0, nt) + [rsk], writes=Cb.keys(c, t0, nt))
                            S.op("act", lambda h, c=c: h.activation(out=Hh.ap(c, t0, nt), in_=Cb.ap(c, t0, nt),
                                                                    func=AF.Silu, bias=P(l, c, R_LNB),
                                                                    scale=P(l, c, R_LNG)),
                                 reads=Cb.keys(c, t0, nt) + [("PRM", l)], writes=Hh.keys(c, t0, nt))
                    Mb = Cb
                    for (t0, nt) in seg.tiles:
                        pss = []
                        for b in range(4):
                            wo, wok = ws.get(8 + b)
                            for cc in range(2):
                                c = 2 * b + cc
                                pm, pmk = next_ps()
                                hk = sum([Hh.keys(k, t0, nt) for k in range(KC)], [])

                                def mmo(h, w=wo, ps=pm, cc=cc):
                                    for k in range(KC):
                                        ins = h.matmul(flat(ps, seg, nt), V(w, k * 256 + cc * 128, [[1, 128]]),
                                                       Hh.ap(k, t0, nt), start=(k == 0), stop=(k == KC - 1))
                                    return ins
                                S.op("pe", mmo, reads=hk + [wok], writes=[pmk])
                                S.op("act", lambda h, c=c, pm=pm: h.activation(
                                    out=Mb.ap(c, t0, nt), in_=flat(pm, seg, nt), func=AF.Identity,
                                    bias=P(l, c, R_BOUT), scale=1.0),
                                    reads=[pmk, ("PRM", l)], writes=Mb.keys(c, t0, nt))
                        rs, rsk = sumsq_rstd(seg, nt, lambda c: Mb.ap(c, t0, nt), lambda c: Mb.keys(c, t0, nt))
                        postnorm_add(seg, X, Mb, l, R_MIXPOST, t0, nt, rs, rsk)
                else:
                    units = [(8, b_w_grp[j][g], 2, ("wgrp", j, g)) for g in range(4)]
                    ws = WStream(units, 3)
                    ws.get(0)
                    if isS:
                        pass
                    elif seg.first:
                        S.op("pool", lambda h: h.memset(V(R2, 0, [[Hf.CS, KC], [1, 15]]), 0.0),
                             writes=R2ALL)
                    else:
                        S.op("act", lambda h: h.activation(out=V(R2, 0, [[Hf.CS, KC], [1, 15]]),
                                                           in_=V(PST, j * KC * 15, [[15, KC], [1, 15]]),
                                                           func=AF.Copy),
                             reads=[("PST", j)], writes=R2ALL)
                    prenorm(seg, X, Hf, l, R_MIXPRE)
                    if si == 0 and l + 1 < NL:
                        load_params(l + 1)
                    if isS:
                        store_state_S(Hf, 15, spool[j], nps[j])
                    elif seg.last:
                        store_T(lambda c: Hf.ap(c, L - 15, 15), lambda c: Hf.keys(c, L - 15, 15), 15, KC,
                                [(npp[j], 0, 15)])
                    else:
                        S.op("act", lambda h: h.activation(out=V(PST, j * KC * 15, [[15, KC], [1, 15]]),
                                                           in_=V(R2, Hf.halo + L - 15, [[Hf.CS, KC], [1, 15]]),
                                                           func=AF.Copy),
                             reads=sum([Hf.keys(c, L - 15, 15) for c in range(KC)], []), writes=[("PST", j)])
                    WSZ = ns * (15 + L)
                    for c in range(KC):
                        g = c // 2
                        w = POOLW[g]
                        par = c % 2
                        if par == 0:
                            wb = [Buf(R2, "PWa0", seg, 15, base=KC * Hf.CS),
                                  Buf(R2, "PWb0", seg, 15, base=KC * Hf.CS + WSZ)]
                        else:
                            wb = [Buf(R3, "PWa1", seg, 15, base=0), Buf(R3, "PWb1", seg, 15, base=WSZ)]
                        allk = lambda bf: [(bf.name, 0, ti) for ti in range(len(seg.tiles))] + [(bf.name, 0, "h")]
                        src, srck = Hf, [("Hf", c, ti) for ti in range(len(seg.tiles))] + [("Hf", c, "h")]
                        srcc = c
                        lo = -15
                        sh = 1
                        step = 0
                        while sh < w:
                            dstb = wb[step % 2]
                            lo2 = lo + sh
                            nn = L - lo2
                            S.op("pool" if par else "dve", lambda h, src=src, srcc=srcc, dstb=dstb, lo2=lo2, nn=nn, sh=sh: h.tensor_tensor(
                                out=dstb.ap(0, lo2, nn), in0=src.ap(srcc, lo2, nn), in1=src.ap(srcc, lo2 - sh, nn),
                                op=ALU.add), reads=srck, writes=allk(dstb))
                            src, srck, srcc = dstb, allk(dstb), 0
                            lo = lo2
                            sh *= 2
                            step += 1
                        hkeys = sum([Hh.keys(c, t0, nt) for (t0, nt) in seg.tiles], [])
                        S.op("dve", lambda h, src=src, c=c, w=w: h.scalar_tensor_tensor(
                            out=Hh.ap(c, 0, L), in0=src.ap(0, 0, L), scalar=1.0 / w, in1=Hf.ap(c, 0, L),
                            op0=ALU.mult, op1=ALU.subtract),
                            reads=srck + [("Hf", c, ti) for ti in range(len(seg.tiles))], writes=hkeys)
                        if seg.first:
                            tt, ttk = rot(TT, "TT")
                            S.op("dve", lambda h, src=src, g=g, tt=tt: h.tensor_tensor(
                                out=V(tt, 0, [[1, 16]]), in0=src.ap(0, 0, 16), in1=IC[:, 16 * g:16 * g + 16],
                                op=ALU.mult), reads=srck + [("IC",)], writes=[ttk])
                            S.op("dve", lambda h, c=c, tt=tt: h.tensor_tensor(
                                out=Hh.ap(c, 0, 16), in0=V(tt, 0, [[1, 16]]), in1=Hf.ap(c, 0, 16),
                                op=ALU.subtract), reads=[ttk, ("Hf", c, 0)], writes=Hh.keys(c, 0, 16))
                    Mb = Cb
                    for (t0, nt) in seg.tiles:
                        for g in range(4):
                            wgp, wgpk = ws.get(g)
                            for cc in range(2):
                                c = 2 * g + cc
                                pm, pmk = next_ps()
                                hk = Hh.keys(2 * g, t0, nt) + Hh.keys(2 * g + 1, t0, nt)

                                def mmg(h, w=wgp, ps=pm, cc=cc, g=g):
                                    for k in range(2):
                                        ins = h.matmul(flat(ps, seg, nt), V(w, k * 256 + cc * 128, [[1, 128]]),
                                                       Hh.ap(2 * g + k, t0, nt), start=(k == 0), stop=(k == 1))
                                    return ins
                                S.op("pe", mmg, reads=hk + [wgpk], writes=[pmk])
                                S.op("act", lambda h, c=c, pm=pm: h.activation(
                                    out=Mb.ap(c, t0, nt), in_=flat(pm, seg, nt), func=AF.Identity,
                                    scale=P(l, c, R_SCALE)),
                                    reads=[pmk, ("PRM", l)], writes=Mb.keys(c, t0, nt) + PW1ALL)
                        rs, rsk = sumsq_rstd(seg, nt, lambda c: Mb.ap(c, t0, nt), lambda c: Mb.keys(c, t0, nt))
                        postnorm_add(seg, X, Mb, l, R_MIXPOST, t0, nt, rs, rsk)

                units = []
                for b in range(11):
                    units.append((8, f_w_up[l][:, 256 * b:256 * b + 256], KC, ("wup", l, b, 0)))
                    units.append((8, f_w_up[l][:, DFF + 256 * b:DFF + 256 * b + 256], KC, ("wup", l, b, 1)))
                for b in range(8):
                    units.append((22, f_w_down[l][:, 128 * b:128 * b + 128], JC, ("wdn", l, b)))
                ws = WStream(units, 3)
                ws.get(0)
                if isS:
                    pass
                elif seg.first:
                    S.op("pool", lambda h: h.memset(V(R1b, 0, [[Hh.CS, KC], [1, 2]]), 0.0),
                         writes=[("H", c, "h") for c in range(KC)])
                else:
                    S.op("act", lambda h: h.activation(out=V(R1b, 0, [[Hh.CS, KC], [1, 2]]),
                                                       in_=V(FH, l * KC * 2, [[2, KC], [1, 2]]), func=AF.Copy),
                         reads=[("FH", l)], writes=[("H", c, "h") for c in range(KC)])
                prenorm(seg, X, Hh, l, R_FFNPRE)
                if (not isS) and (not seg.last):
                    S.op("act", lambda h: h.activation(out=V(FH, l * KC * 2, [[2, KC], [1, 2]]),
                                                       in_=V(R1b, Hh.halo + L - 2, [[Hh.CS, KC], [1, 2]]),
                                                       func=AF.Copy),
                         reads=sum([Hh.keys(c, L - 2, 2) for c in range(KC)], []), writes=[("FH", l)])
                for b in range(11):
                    wgt, wgtk = ws.get(2 * b)
                    wvl, wvlk = ws.get(2 * b + 1)
                    for ti, (t0, nt) in enumerate(seg.tiles):
                        ntp = nt + 2
                        lastt = (ti == len(seg.tiles) - 1)
                        for cc in range(2):
                            jj = 2 * b + cc
                            res = []
                            for which, (w, wk) in enumerate(((wgt, wgtk), (wvl, wvlk))):
                                uch = jj + which * JC
                                ps, pk = next_ps()
                                hk = sum([Hh.keys(k, t0 - 2, ntp) for k in range(KC)], [])
                                if isS:
                                    pfull = V(ps, 0, [[ntp, ns], [1, ntp]])
                                    pmain = V(ps, 2, [[ntp, ns], [1, nt]])
                                    phalo = V(ps, 0, [[ntp, ns], [1, 2]])

                                    def mmu(h, w=w, cc=cc, pmain=pmain, phalo=phalo, uch=uch):
                                        for k in range(KC):
                                            h.matmul(pmain, V(w, k * 256 + cc * 128, [[1, 128]]),
                                                     Hh.ap(k, t0, nt), start=(k == 0), stop=(k == KC - 1))
                                        return h.matmul(phalo, IDN[:], V(UST, uch * 32, [[2, ns], [1, 2]]),
                                                        start=True, stop=True)
                                    S.op("pe", mmu, reads=hk + [wk, ("UST",), ("IDN",)], writes=[pk])
                                else:
                                    pfull = V(ps, 0, [[1, ntp]])

                                    def mmu(h, w=w, cc=cc, pfull=pfull):
                                        for k in range(KC):
                                            ins = h.matmul(pfull, V(w, k * 256 + cc * 128, [[1, 128]]),
                                                           Hh.ap(k, t0 - 2, ntp), start=(k == 0), stop=(k == KC - 1))
                                        return ins
                                    S.op("pe", mmu, reads=hk + [wk], writes=[pk])

                                def pv(sh, ps=ps):
                                    if isS:
                                        return V(ps, sh, [[ntp, ns], [1, nt]])
                                    return V(ps, sh, [[1, nt]])
                                acc, acck = rot(GS if which == 0 else VS, "GS" if which == 0 else "VS")
                                if isS and which == 1:
                                    S.op("dve", lambda h, acc=acc, pv=pv, uch=uch: h.tensor_scalar(
                                        out=flat(acc, seg, nt), in0=pv(2), scalar1=FP(l, uch, 2), scalar2=FP(l, uch, 3),
                                        op0=ALU.mult, op1=ALU.add),
                                        reads=[pk, ("FPRM", l)], writes=[acck])
                                else:
                                    S.op("act", lambda h, acc=acc, pv=pv, uch=uch: h.activation(
                                        out=flat(acc, seg, nt), in_=pv(2), func=AF.Identity,
                                        bias=FP(l, uch, 3), scale=FP(l, uch, 2)),
                                        reads=[pk, ("FPRM", l)], writes=[acck])
                                for tap in (1, 0):
                                    S.op("dve", lambda h, acc=acc, pv=pv, uch=uch, tap=tap: h.scalar_tensor_tensor(
                                        out=flat(acc, seg, nt), in0=pv(tap), scalar=FP(l, uch, tap),
                                        in1=flat(acc, seg, nt), op0=ALU.mult, op1=ALU.add),
                                        reads=[pk, acck, ("FPRM", l)], writes=[acck])
                                if lastt and seg.last:
                                    src = V(ps, nt, [[ntp, ns], [1, 2]]) if isS else V(ps, nt, [[1, 2]])
                                    dst = V(UO, uch * 32, [[2, ns], [1, 2]]) if isS else V(UO, uch * 32, [[1, 2]])
                                    S.op("act", lambda h, src=src, dst=dst: h.activation(out=dst, in_=src, func=AF.Copy),
                                         reads=[pk], writes=[("UO", uch)])
                                res.append((acc, acck))
                            (ga, gak), (va, vak) = res
                            S.op("act", lambda h, ga=ga: h.activation(out=flat(ga, seg, nt), in_=flat(ga, seg, nt),
                                                                      func=AF.Silu), reads=[gak], writes=[gak])
                            S.op("pool", lambda h, ga=ga, va=va, jj=jj: h.tensor_tensor(
                                out=A.ap(jj, t0, nt), in0=flat(ga, seg, nt), in1=flat(va, seg, nt), op=ALU.mult),
                                reads=[gak, vak], writes=A.keys(jj, t0, nt))
                if seg.last:
                    R = 32 if isS else 2
                    dstT = nfs if isS else nfp
                    for q in range(6):
                        c0 = q * 8
                        n = min(8, UC - c0)
                        store_T(lambda c, c0=c0, R=R: V(UO, (c0 + c) * 32, [[1, R]]),
                                lambda c, c0=c0: [("UO", c0 + c)], R, n,
                                [(dstT[l][:, c0 * 128:(c0 + n) * 128], 0, R)])
                Fb = Cb
                for b in range(8):
                    wd, wdk = ws.get(22 + b)
                    for (t0, nt) in seg.tiles:
                        m = b
                        pf, pfk = next_ps()
                        ak = sum([A.keys(q, t0, nt) for q in range(JC)], [])

                        def mmd(h, w=wd, ps=pf, t0=t0, nt=nt):
                            for q in range(JC):
                                ins = h.matmul(flat(ps, seg, nt), V(w, q * 128, [[1, 128]]),
                                               A.ap(q, t0, nt), start=(q == 0), stop=(q == JC - 1))
                            return ins
                        S.op("pe", mmd, reads=ak + [wdk], writes=[pfk])
                        S.op("act", lambda h, m=m, pf=pf, t0=t0, nt=nt: h.activation(
                            out=Fb.ap(m, t0, nt), in_=flat(pf, seg, nt), func=AF.Copy),
                            reads=[pfk], writes=Fb.keys(m, t0, nt))
                if isS and l + 1 < NL:
                    load_halo_S(l + 1)
                for (t0, nt) in seg.tiles:
                    rs, rsk = sumsq_rstd(seg, nt, lambda c: Fb.ap(c, t0, nt), lambda c: Fb.keys(c, t0, nt))
                    postnorm_add(seg, X, Fb, l, R_FFNPOST, t0, nt, rs, rsk)

            Yb = Cb
            for (t0, nt) in seg.tiles:
                rs, rsk = sumsq_rstd(seg, nt, lambda c: X.ap(c, t0, nt), lambda c: X.keys(c, t0, nt))
                for c in range(KC):
                    S.op("dve", lambda h, c=c: h.scalar_tensor_tensor(
                        out=Yb.ap(c, t0, nt), in0=X.ap(c, t0, nt), scalar=P(0, c, R_FINAL),
                        in1=flat(rs, seg, nt), op0=ALU.mult, op1=ALU.mult),
                        reads=X.keys(c, t0, nt) + [rsk, ("PRM", 0)], writes=Yb.keys(c, t0, nt))
            if isS:
                store_T(lambda c: V(R3, c * Yb.CS, [[1, 128]]), lambda c: [("C", c, 0)], 128, KC,
                        [(ys[:, :], 0, 128)])
            else:
                p0 = seg.idx * LP
                r = 0
                while r < LP:
                    nr = min(128, LP - r)
                    a = p0 + r
                    pieces = []
                    if a + nr > NMETA:
                        skip = max(0, NMETA - a)
                        pieces.append((yp[a + skip - NMETA:a + nr - NMETA, :], skip, nr - skip))
                    if pieces:
                        store_T(lambda c, r=r, nr=nr: Yb.ap(c, r, nr), lambda c, r=r, nr=nr: Yb.keys(c, r, nr),
                                nr, KC, pieces)
                    r += nr
        S.finish()
    return nc


def _prep(inputs):
    f = lambda k: np.ascontiguousarray(np.asarray(inputs[k], dtype=np.float32))
    prm = np.zeros((NL, PR, D), np.float32)
    for l in range(NL):
        j = l // 2
        prm[l, R_MIXPRE] = f("mix_pre")[l]
        prm[l, R_MIXPOST] = f("mix_post")[l]
        prm[l, R_FFNPRE] = f("ffn_pre")[l]
        prm[l, R_FFNPOST] = f("ffn_post")[l]
        prm[l, R_FINAL] = f("final_norm")
        if l % 2 == 0:
            prm[l, R_BU] = f("a_b_in")[j, :D]
            prm[l, R_BG] = f("a_b_in")[j, D:]
            prm[l, R_BDW] = f("a_b_dw")[j]
            prm[l, R_LNG] = f("a_ln_g")[j]
            prm[l, R_LNB] = f("a_ln_b")[j]
            prm[l, R_BOUT] = f("a_b_out")[j]
            prm[l, R_WDW:R_WDW + 31] = f("a_w_dw")[j]
        else:
            prm[l, R_SCALE] = f("b_scale")[j]
    fprm = np.concatenate([f("f_w_dw"), f("f_b_dw")[:, None, :]], axis=1)
    shared = dict(meta=f("meta_tokens"), prm=prm, fprm=np.ascontiguousarray(fprm),
                  a_w_in=f("a_w_in"), a_w_out=f("a_w_out"), b_w_grp=f("b_w_grp"),
                  f_w_up=f("f_w_up"), f_w_down=f("f_w_down"))
    xpr, xsa = f("x_prompt"), f("x_sample")
    sc, sp_, sf = f("state_conv"), f("state_pool"), f("state_ffn")
    in_maps = []
    for c in range(8):
        m = dict(shared)
        m["xp"] = xpr[c]
        m["xs"] = np.ascontiguousarray(xsa[16 * c:16 * c + 16].reshape(128, D))
        m["sconv"] = np.ascontiguousarray(sc[:, 16 * c:16 * c + 16].reshape(2, 480, D))
        m["spool"] = np.ascontiguousarray(sp_[:, 16 * c:16 * c + 16].reshape(2, 240, D))
        m["sffn"] = np.ascontiguousarray(sf[:, 16 * c:16 * c + 16].reshape(NL, 32, 2 * DFF))
        in_maps.append(m)
    return in_maps


_NC_CACHE = {}


def kernel(**inputs):
    in_maps = _prep(inputs)
    if "nc" not in _NC_CACHE:
        _NC_CACHE["nc"] = build()
    nc = _NC_CACHE["nc"]
    res = run_bass_kernel_spmd(nc, in_maps, core_ids=list(range(8)))
    r = res.results
    y_prompt = np.stack([r[c]["yp"] for c in range(8)], 0)
    y_sample = np.concatenate([r[c]["ys"].reshape(16, 8, D) for c in range(8)], 0)
    ncp = np.stack([r[c]["ncp"] for c in range(8)], 1)
    npp = np.stack([r[c]["npp"] for c in range(8)], 1)
    nfp = np.stack([r[c]["nfp"] for c in range(8)], 1)
    ncs = np.concatenate([r[c]["ncs"].reshape(2, 16, 30, D) for c in range(8)], 1)
    nps = np.concatenate([r[c]["nps"].reshape(2, 16, 15, D) for c in range(8)], 1)
    nfs = np.concatenate([r[c]["nfs"].reshape(NL, 16, 2, 2 * DFF) for c in range(8)], 1)
    return (y_prompt.astype(np.float32), y_sample.astype(np.float32), ncp.astype(np.float32),
            npp.astype(np.float32), nfp.astype(np.float32), ncs.astype(np.float32),
            nps.astype(np.float32), nfs.astype(np.float32))
```

```python
import contextlib
import numpy as np
import concourse.bass as bass
import concourse.mybir as mybir
from concourse.bass_utils import run_bass_kernel_spmd

F32 = mybir.dt.float32
BF16 = mybir.dt.bfloat16
AF = mybir.ActivationFunctionType
ALU = mybir.AluOpType

D = 1024
KC = 8
DFF = 2816
JC = 22
UC = 44
NL = 4
EPS = 1e-6
NMETA = 16
SEQ = 2048
LP = 688
NT = 344
NPSEG = 3
PR = 48
R_MIXPRE, R_MIXPOST, R_FFNPRE, R_FFNPOST, R_BU, R_BG, R_BDW, R_LNG, R_LNB, R_BOUT, R_SCALE, R_FINAL, R_WDW = \
    0, 1, 2, 3, 4, 5, 6, 7, 8, 9, 10, 11, 12
POOLW = (2, 4, 8, 16)
SEM_EPOCH = 30000
DMA_SLOTS = 8


class _Eng:
    def __init__(self, h, name):
        self.h = h
        self.name = name
        self.sem = None
        self.n = 0
        self.epoch = -1
        self.seen = {}


class Sched:
    def __init__(self, nc, es):
        self.nc = nc
        self.es = es
        self.eng = {}
        for name, h in (("pe", nc.tensor), ("act", nc.scalar), ("dve", nc.vector),
                        ("pool", nc.gpsimd), ("sp", nc.sync)):
            self.eng[name] = _Eng(h, name)
        self.lastw = {}
        self.readers = {}
        self.dq = {}
        self.nsem = 0

    def _newsem(self, nm):
        self.nsem += 1
        return self.es.enter_context(self.nc.semaphore(f"{nm}_{self.nsem}"))

    def add_dma_queue(self, qn, en):
        self.dq[qn] = dict(en=en, sems=[self._newsem(qn) for _ in range(DMA_SLOTS)], i=0)

    def _collect(self, reads, writes):
        d = {}

        def add(ev):
            if ev is None:
                return
            o = d.get(ev[0])
            if o is None or o[2] < ev[2]:
                d[ev[0]] = ev
        for k in reads:
            add(self.lastw.get(k))
        for k in writes:
            add(self.lastw.get(k))
            rd = self.readers.get(k)
            if rd:
                for ev in rd.values():
                    add(ev)
        return d

    def _wait(self, en, d):
        e = self.eng[en]
        for name, sem, val in d.values():
            if en == "pe" and name.startswith("pe#"):
                continue
            if e.seen.get(name, 0) >= val:
                continue
            e.h.wait_ge(sem, val)
            e.seen[name] = val

    def _commit(self, ev, reads, writes):
        for k in writes:
            self.lastw[k] = ev
            self.readers[k] = {}
        for k in reads:
            self.readers.setdefault(k, {})[ev[0]] = ev

    def op(self, en, fn, reads=(), writes=()):
        e = self.eng[en]
        self._wait(en, self._collect(reads, writes))
        if e.sem is None or e.n >= SEM_EPOCH:
            e.epoch += 1
            e.n = 0
            e.sem = self._newsem(en)
        ins = fn(e.h)
        e.n += 1
        ins.then_inc(e.sem, 1)
        e.last_ev = (f"{en}#{e.epoch}", e.sem, e.n)
        self._commit(e.last_ev, reads, writes)

    def barrier(self):
        evs = {}
        for en, e in self.eng.items():
            ev = getattr(e, "last_ev", None)
            if ev is not None:
                evs[ev[0]] = ev
        for qn, q in self.dq.items():
            for slot in range(DMA_SLOTS):
                cnt = (q["i"] - slot + DMA_SLOTS - 1) // DMA_SLOTS
                if cnt > 0:
                    evs[f"{qn}{slot}"] = (f"{qn}{slot}", q["sems"][slot], 16 * cnt)
        for en in self.eng:
            e = self.eng[en]
            for name, sem, val in evs.values():
                if e.seen.get(name, 0) >= val:
                    continue
                e.h.wait_ge(sem, val)
                e.seen[name] = val
        self.lastw.clear()
        self.readers.clear()

    def dma(self, qn, out, in_, reads=(), writes=()):
        q = self.dq[qn]
        en = q["en"]
        e = self.eng[en]
        slot = q["i"] % DMA_SLOTS
        rnd = q["i"] // DMA_SLOTS
        q["i"] += 1
        name = f"{qn}{slot}"
        sem = q["sems"][slot]
        d = self._collect(reads, writes)
        if rnd > 0:
            d[name] = (name, sem, 16 * rnd)
        self._wait(en, d)
        e.h.dma_start(out=out, in_=in_).then_inc(sem, 16)
        self._commit((name, sem, 16 * (rnd + 1)), reads, writes)

    def finish(self):
        for qn, q in self.dq.items():
            e = self.eng[q["en"]]
            for slot in range(DMA_SLOTS):
                cnt = (q["i"] - slot + DMA_SLOTS - 1) // DMA_SLOTS
                if cnt > 0:
                    e.h.wait_ge(q["sems"][slot], 16 * cnt)


class Seg:
    def __init__(self, kind, idx):
        self.kind = kind
        self.idx = idx
        if kind == "P":
            self.ns, self.L = 1, LP
            self.tiles = [(0, NT), (NT, NT)]
        else:
            self.ns, self.L = 16, 8
            self.tiles = [(0, 8)]
        self.first = (kind == "P" and idx == 0)
        self.last = (kind == "S") or (idx == NPSEG - 1)


class Buf:
    def __init__(self, t, name, seg, halo, base=0):
        self.t = t
        self.rl = t.shape[1]
        self.name = name
        self.ns = seg.ns
        self.halo = halo
        self.SS = halo + seg.L
        self.CS = seg.ns * self.SS
        self.base = base
        self.L = seg.L
        self.ntl = NT if seg.kind == "P" else seg.L

    def ap(self, c, t0, nt, nch=1):
        off = self.base + c * self.CS + self.halo + t0
        dims = [[self.rl, 128]]
        if nch > 1:
            dims.append([self.CS, nch])
        if self.ns > 1:
            dims.append([self.SS, self.ns])
        dims.append([1, nt])
        return bass.AP(self.t, off, dims)

    def keys(self, c, t0, nt, nch=1):
        lo = max(t0, 0) // self.ntl
        hi = max(t0 + nt - 1, 0) // self.ntl
        ks = []
        for cc in range(c, c + nch):
            for ti in range(lo, hi + 1):
                ks.append((self.name, cc, ti))
            if t0 < 0:
                ks.append((self.name, cc, "h"))
        return ks


def build(nseg_list=("P0", "P1", "P2", "S")):
    nc = bass.Bass("TRN2", target_bir_lowering=False)

    def din(name, shape):
        return nc.dram_tensor(name, list(shape), F32, kind="ExternalInput").ap()

    def dout(name, shape):
        return nc.dram_tensor(name, list(shape), F32, kind="ExternalOutput").ap()

    xp = din("xp", (SEQ, D))
    xs = din("xs", (128, D))
    sconv = din("sconv", (2, 480, D))
    spool = din("spool", (2, 240, D))
    sffn = din("sffn", (NL, 32, 2 * DFF))
    meta = din("meta", (NMETA, D))
    prm = din("prm", (NL, PR, D))
    fprm = din("fprm", (NL, 4, 2 * DFF))
    a_w_in = din("a_w_in", (2, D, 2 * D))
    a_w_out = din("a_w_out", (2, D, D))
    b_w_grp = din("b_w_grp", (2, 4, 256, 256))
    f_w_up = din("f_w_up", (NL, D, 2 * DFF))
    f_w_down = din("f_w_down", (NL, DFF, D))
    yp = dout("yp", (SEQ, D))
    ys = dout("ys", (128, D))
    ncp = dout("ncp", (2, 30, D))
    npp = dout("npp", (2, 15, D))
    nfp = dout("nfp", (NL, 2, 2 * DFF))
    ncs = dout("ncs", (2, 480, D))
    nps = dout("nps", (2, 240, D))
    nfs = dout("nfs", (NL, 32, 2 * DFF))

    with contextlib.ExitStack() as es:
        S = Sched(nc, es)
        S.add_dma_queue("qi", "sp")
        S.add_dma_queue("qo", "sp")
        S.add_dma_queue("qw", "pool")
        S.add_dma_queue("qs", "sp")
        WSCR = nc.dram_tensor("wscr", [152, 128, JC * 128], BF16).ap()

        def sb(name, cols, dt=F32):
            return es.enter_context(nc.sbuf_tensor(name, [128, cols], dt))

        XB = sb("XB", KC * LP)
        R1 = sb("R1", KC * (LP + 2) // 2)
        R1b = R1.bitcast(BF16)
        R2 = sb("R2", JC * LP // 2)
        R2b = R2.bitcast(BF16)
        R3 = sb("R3", KC * LP)
        NW8, NW22 = 5, 4
        W8 = [sb(f"W8_{i}", KC * 256, BF16) for i in range(NW8)]
        W22 = [sb(f"W22_{i}", JC * 128, BF16) for i in range(NW22)]
        STI = [sb(f"STI{i}", 1024) for i in range(3)]
        STO = [sb(f"STO{i}", 1024) for i in range(2)]
        PRM = sb("PRM", NL * KC * PR)
        FPRM = sb("FPRM", NL * UC * 4)
        IDN = sb("IDN", 128)
        ONES = sb("ONES", 128, BF16)
        NSCR = 2
        SQ = [sb(f"SQ{i}", NT, BF16) for i in range(3)]
        CB16 = [sb(f"CB16_{i}", NT, BF16) for i in range(NSCR)]
        RS = [sb(f"RS{i}", NT) for i in range(NSCR)]
        TT = [sb(f"TT{i}", NT) for i in range(3)]
        GS = [sb(f"GS{i}", NT) for i in range(3)]
        VS = [sb(f"VS{i}", NT) for i in range(3)]
        CST = sb("CST", 2 * KC * 30, BF16)
        VFT = sb("VFT", KC * 128)
        DG = [sb(f"DG{i}", 31 * 128, BF16) for i in range(2)]
        PST = sb("PST", 2 * KC * 15)
        FH = sb("FH", NL * KC * 2, BF16)
        UST = sb("UST", UC * 32)
        UO = sb("UO", UC * 32)
        CT = [sb(f"CT{i}", 128) for i in range(4)]
        IC = sb("IC", 4 * 16)
        PS = [es.enter_context(nc.psum_tensor(f"PS{i}", [128, 512], F32)) for i in range(8)]

        st = dict(ps=0, w8=0, w22=0, sti=0, sto=0, scr={})

        def V(t, off, dims, parts=128):
            return bass.AP(t, off, [[t.shape[1], parts]] + [list(x) for x in dims])

        def next_ps():
            i = st["ps"] % 8
            st["ps"] += 1
            return PS[i], ("ps", i)

        def rot(lst, nm):
            i = st["scr"].get(nm, 0)
            st["scr"][nm] = i + 1
            j = i % len(lst)
            return lst[j], (nm, j)

        S.op("pool", lambda h: h.memset(IDN[:], 0.0), writes=[("IDN",)])
        S.op("pool", lambda h: h.affine_select(out=IDN[:], in_=IDN[:], pattern=[[-1, 128]],
                                               compare_op=ALU.not_equal, fill=1.0, base=0,
                                               channel_multiplier=1), reads=[("IDN",)], writes=[("IDN",)])
        S.op("pool", lambda h: h.memset(ONES[:], 1.0), writes=[("ONES",)])
        for g, w in enumerate(POOLW):
            S.op("pool", lambda h: h.memset(IC[:, 16 * g:16 * g + 16], 1.0 / w), writes=[("IC",)])
            for t in range(w - 1):
                S.op("pool", lambda h: h.memset(IC[:, 16 * g + t:16 * g + t + 1], 1.0 / (t + 1)),
                     writes=[("IC",)])

        def load_T(src_rows, R, ncol_chunks, dst_fn, dst_keys_fn):
            stg, sk = rot(STI, "sti")
            W = ncol_chunks * 128
            for (ap, r0, nr) in src_rows:
                S.dma("qi", V(stg, r0 * 1024, [[1, W]], parts=nr), ap, writes=[sk])
            per = max(1, min(ncol_chunks, 512 // R))
            c0 = 0
            while c0 < ncol_chunks:
                n = min(per, ncol_chunks - c0)
                ps, pk = next_ps()

                def fn(h, c0=c0, n=n, ps=ps):
                    for i in range(n):
                        ins = h.transpose(V(ps, i * R, [[1, R]]),
                                          V(stg, (c0 + i) * 128, [[1, 128]], parts=R),
                                          V(IDN, 0, [[1, R]], parts=R))
                    return ins
                S.op("pe", fn, reads=[sk, ("IDN",)], writes=[pk])
                dst = dst_fn(c0, n)
                shp = list(dst.shape)[1:]
                dims, stp = [], 1
                for cnt in reversed(shp):
                    dims.insert(0, [stp, cnt])
                    stp *= cnt
                src = V(ps, 0, dims)
                S.op("act", lambda h, dst=dst, src=src: h.activation(out=dst, in_=src, func=AF.Copy),
                     reads=[pk], writes=dst_keys_fn(c0, n))
                c0 += n

        def store_T(src_fn, src_keys_fn, R, ncol_chunks, dst_pieces):
            stg, sk = rot(STO, "sto")
            c0 = 0
            while c0 < ncol_chunks:
                n = min(4, ncol_chunks - c0)
                ps, pk = next_ps()

                def fn(h, c0=c0, n=n, ps=ps):
                    for i in range(n):
                        ins = h.transpose(V(ps, i * 128, [[1, 128]], parts=R), src_fn(c0 + i), IDN[:])
                    return ins
                rk = []
                for i in range(n):
                    rk += src_keys_fn(c0 + i)
                S.op("pe", fn, reads=rk + [("IDN",)], writes=[pk])
                S.op("act", lambda h, c0=c0, n=n, ps=ps: h.activation(
                    out=V(stg, c0 * 128, [[1, n * 128]], parts=R),
                    in_=V(ps, 0, [[1, n * 128]], parts=R), func=AF.Copy),
                    reads=[pk], writes=[sk])
                c0 += n
            W = ncol_chunks * 128
            for (ap, r0, nr) in dst_pieces:
                S.dma("qo", ap, V(stg, r0 * 1024, [[1, W]], parts=nr), reads=[sk])

        def store_state_S(B, Rr, src_d, dst_d, contig=None, ckeys=None):
            keep = Rr - 8
            S.dma("qo", dst_d.rearrange("(s r) d -> s r d", r=Rr)[:, 0:keep, :],
                  src_d.rearrange("(s r) d -> s r d", r=Rr)[:, 8:Rr, :])
            stg, sk = rot(STO, "sto")
            for c0 in range(0, KC, 4):
                ps, pk = next_ps()
                cts = []
                for i in range(4):
                    if contig is not None:
                        cts.append((contig(c0 + i), ckeys(c0 + i)))
                        continue
                    ct, ctk = rot(CT, "CT")
                    S.op("act", lambda h, ct=ct, i=i: h.activation(out=V(ct, 0, [[8, 16], [1, 8]]),
                                                                   in_=B.ap(c0 + i, 0, 8), func=AF.Copy),
                         reads=B.keys(c0 + i, 0, 8), writes=[ctk])
                    cts.append((ct[:], [ctk]))

                def fn(h, ps=ps, cts=cts):
                    for i, (cta, _) in enumerate(cts):
                        ins = h.transpose(V(ps, i * 128, [[1, 128]]), cta, IDN[:])
                    return ins
                S.op("pe", fn, reads=sum([k for _, k in cts], []) + [("IDN",)], writes=[pk])
                S.op("act", lambda h, ps=ps, c0=c0: h.activation(out=V(stg, c0 * 128, [[1, 512]]),
                                                                 in_=V(ps, 0, [[1, 512]]), func=AF.Copy),
                     reads=[pk], writes=[sk])
            for s_ in range(16):
                S.dma("qo", dst_d[s_ * Rr + keep:(s_ + 1) * Rr, :], V(stg, s_ * 8 * 1024, [[1, 1024]], parts=8),
                      reads=[sk])

        def load_params(l):
            load_T([(prm[l], 0, PR)], PR, KC,
                   lambda c0, n: V(PRM, (l * KC + c0) * PR, [[PR, n], [1, PR]]),
                   lambda c0, n: [("PRM", l)])
            for q in range(6):
                c0 = q * 8
                n = min(8, UC - c0)
                load_T([(fprm[l][:, c0 * 128:(c0 + n) * 128], 0, 4)], 4, n,
                       lambda cc, nn, c0=c0: V(FPRM, (l * UC + c0 + cc) * 4, [[4, nn], [1, 4]]),
                       lambda cc, nn: [("FPRM", l)])

        load_params(0)

        def P(l, c, r):
            return V(PRM, (l * KC + c) * PR + r, [[1, 1]])

        def FP(l, c, r):
            return V(FPRM, (l * UC + c) * 4 + r, [[1, 1]])

        scr_map = {}

        def _load_w(t, key, n, cast_out, cast_in, uid):
            if uid not in scr_map:
                u = scr_map[uid] = len(scr_map)
                S.dma("qw", cast_out, cast_in, writes=[key])
                S.dma("qs", WSCR[u][:, 0:n], V(t, 0, [[1, n]]), reads=[key], writes=[("scr", u)])
            else:
                u = scr_map[uid]
                S.dma("qs", V(t, 0, [[1, n]]), WSCR[u][:, 0:n], reads=[("scr", u)], writes=[key])

        def load_w8(src_ap, kch, uid):
            i = st["w8"] % NW8
            st["w8"] += 1
            t = W8[i]
            _load_w(t, ("w8", i), kch * 256, V(t, 0, [[256, kch], [1, 256]]),
                    src_ap.rearrange("(k p) n -> p k n", p=128), uid)
            return t, ("w8", i)

        def load_w22(src_ap, uid):
            i = st["w22"] % NW22
            st["w22"] += 1
            t = W22[i]
            _load_w(t, ("w22", i), JC * 128, V(t, 0, [[128, JC], [1, 128]]),
                    src_ap.rearrange("(k p) n -> p k n", p=128), uid)
            return t, ("w22", i)

        class WStream:
            def __init__(self, units, ahead):
                self.units = units
                self.loaded = []
                self.nxt = 0
                self.ahead = ahead

            def _issue(self):
                kind, src, kch, uid = self.units[self.nxt]
                self.loaded.append(load_w8(src, kch, uid) if kind == 8 else load_w22(src, uid))
                self.nxt += 1

            def get(self, i):
                while self.nxt < len(self.units) and self.nxt <= i + self.ahead:
                    self._issue()
                return self.loaded[i]

        def tile_cols(seg, nt):
            return seg.ns * nt

        def flat(t, seg, nt, off=0):
            if seg.ns > 1:
                return V(t, off, [[nt, seg.ns], [1, nt]])
            return V(t, off, [[1, nt]])

        def sumsq_rstd(seg, nt, sq_src_fn, sq_keys_fn, sq_scale=None, sq_bias=None):
            ps, pk = next_ps()
            n = tile_cols(seg, nt)
            for c in range(KC):
                sq, sqk = rot(SQ, "SQ")
                kw = {}
                if sq_scale is not None:
                    kw["scale"] = sq_scale(c)
                if sq_bias is not None:
                    kw["bias"] = sq_bias(c)
                if False:
                    S.op("pool", lambda h, c=c, sq=sq: h.tensor_tensor(
                        out=flat(sq, seg, nt), in0=sq_src_fn(c), in1=sq_src_fn(c), op=ALU.mult),
                        reads=sq_keys_fn(c), writes=[sqk])
                else:
                    S.op("act", lambda h, c=c, sq=sq, kw=kw: h.activation(
                        out=flat(sq, seg, nt), in_=sq_src_fn(c), func=AF.Square, **kw),
                        reads=sq_keys_fn(c), writes=[sqk])
                S.op("pe", lambda h, c=c, sq=sq, ps=ps: h.matmul(
                    V(ps, 0, [[1, n]]), ONES[:], V(sq, 0, [[1, n]]), start=(c == 0), stop=(c == KC - 1)),
                    reads=[sqk, ("ONES",)], writes=[pk])
            rs, rsk = rot(RS, "RS")
            S.op("act", lambda h: h.activation(out=V(rs, 0, [[1, n]]), in_=V(ps, 0, [[1, n]]), func=AF.Ln,
                                               bias=EPS, scale=1.0 / D), reads=[pk], writes=[rsk])
            S.op("act", lambda h: h.activation(out=V(rs, 0, [[1, n]]), in_=V(rs, 0, [[1, n]]), func=AF.Exp,
                                               scale=-0.5), reads=[rsk], writes=[rsk])
            return rs, rsk

        def prenorm(seg, X, dst, l, grow):
            for (t0, nt) in seg.tiles:
                rs, rsk = sumsq_rstd(seg, nt, lambda c: X.ap(c, t0, nt), lambda c: X.keys(c, t0, nt))
                for c in range(KC):
                    S.op("dve", lambda h, c=c: h.scalar_tensor_tensor(
                        out=dst.ap(c, t0, nt), in0=X.ap(c, t0, nt), scalar=P(l, c, grow),
                        in1=flat(rs, seg, nt), op0=ALU.mult, op1=ALU.mult),
                        reads=X.keys(c, t0, nt) + [rsk, ("PRM", l)], writes=dst.keys(c, t0, nt))

        def postnorm_add(seg, X, M, l, grow, t0, nt, rs, rsk):
            for c in range(KC):
                tt, ttk = rot(TT, "TT")
                S.op("pool" if c >= 4 else "dve", lambda h, c=c, tt=tt: h.tensor_tensor(
                    out=flat(tt, seg, nt), in0=M.ap(c, t0, nt), in1=flat(rs, seg, nt), op=ALU.mult),
                    reads=M.keys(c, t0, nt) + [rsk], writes=[ttk])
                S.op("dve", lambda h, c=c, tt=tt: h.scalar_tensor_tensor(
                    out=X.ap(c, t0, nt), in0=flat(tt, seg, nt), scalar=P(l, c, grow), in1=X.ap(c, t0, nt),
                    op0=ALU.mult, op1=ALU.add),
                    reads=X.keys(c, t0, nt) + [ttk, ("PRM", l)], writes=X.keys(c, t0, nt))

        segs = []
        for nm in nseg_list:
            segs.append(Seg("S", 0) if nm == "S" else Seg("P", int(nm[1])))

        for si, seg in enumerate(segs):
            if si > 0 and segs[si - 1].kind != seg.kind:
                S.barrier()
            ns, L = seg.ns, seg.L
            isS = seg.kind == "S"
            ntl_ = len(seg.tiles)
            R2ALL = [("A", q, ti) for q in range(JC) for ti in range(ntl_)]
            for nm_ in ("V", "Hf"):
                R2ALL += [(nm_, c, ti) for c in range(KC) for ti in list(range(ntl_)) + ["h"]]
            R2ALL += [(nm_, 0, ti) for nm_ in ("PWa0", "PWb0") for ti in list(range(ntl_)) + ["h"]]
            PW1ALL = [(nm_, 0, ti) for nm_ in ("PWa1", "PWb1") for ti in list(range(ntl_)) + ["h"]]
            X = Buf(XB, "X", seg, 0)
            Hh = Buf(R1b, "H", seg, 2)
            A = Buf(R2b, "A", seg, 0)
            Vb = Buf(R2b, "V", seg, 30)
            Hf = Buf(R2, "Hf", seg, 15)
            Cb = Buf(R3, "C", seg, 0)

            def load_halo_S(l_):
                j_ = l_ // 2
                if l_ % 2 == 0:
                    for q in range(4):
                        load_T([(sconv[j_][q * 120:(q + 1) * 120, :], 0, 120)], 120, KC,
                               lambda c0, n, q=q: V(R2b, c0 * Vb.CS + 4 * q * Vb.SS,
                                                    [[Vb.CS, n], [Vb.SS, 4], [1, 30]]),
                               lambda c0, n: R2ALL)
                else:
                    for q in range(2):
                        load_T([(spool[j_][q * 120:(q + 1) * 120, :], 0, 120)], 120, KC,
                               lambda c0, n, q=q: V(R2, c0 * Hf.CS + 8 * q * Hf.SS,
                                                    [[Hf.CS, n], [Hf.SS, 8], [1, 15]]),
                               lambda c0, n: R2ALL)

            if isS:
                load_halo_S(0)
            if isS:
                load_T([(xs[:, :], 0, 128)], 128, KC,
                       lambda c0, n: V(XB, c0 * X.CS, [[X.CS, n], [1, 128]]),
                       lambda c0, n: [("X", c, 0) for c in range(c0, c0 + n)])
            else:
                p0 = seg.idx * LP
                r = 0
                while r < LP:
                    nr = min(128, LP - r)
                    pieces = []
                    a = p0 + r
                    if a < NMETA:
                        nm_ = min(NMETA - a, nr)
                        pieces.append((meta[a:a + nm_, :], 0, nm_))
                        if nr > nm_:
                            pieces.append((xp[0:nr - nm_, :], nm_, nr - nm_))
                    else:
                        pieces.append((xp[a - NMETA:a - NMETA + nr, :], 0, nr))
                    load_T(pieces, nr, KC,
                           lambda c0, n, r=r, nr=nr: V(XB, c0 * X.CS + r, [[X.CS, n], [1, nr]]),
                           lambda c0, n, r=r, nr=nr: sum([X.keys(c, r, nr) for c in range(c0, c0 + n)], []))
                    r += nr

            for l in range(NL):
                j = l // 2
                if si == 0 and l + 1 < NL:
                    load_params(l + 1)
                if isS:
                    for q in range(6):
                        c0 = q * 8
                        n = min(8, UC - c0)
                        load_T([(sffn[l][:, c0 * 128:(c0 + n) * 128], 0, 32)], 32, n,
                               lambda cc, nn, c0=c0: V(UST, (c0 + cc) * 32, [[32, nn], [1, 32]]),
                               lambda cc, nn: [("UST",)])
                if l % 2 == 0:
                    units = []
                    for b in range(4):
                        units.append((8, a_w_in[j][:, 256 * b:256 * b + 256], KC, ("win", j, b, 0)))
                        units.append((8, a_w_in[j][:, D + 256 * b:D + 256 * b + 256], KC, ("win", j, b, 1)))
                    for b in range(4):
                        units.append((8, a_w_out[j][:, 256 * b:256 * b + 256], KC, ("wout", j, b)))
                    ws = WStream(units, 3)
                    ws.get(0)
                    if isS:
                        pass
                    elif seg.first:
                        S.op("pool", lambda h: h.memset(V(R2b, 0, [[Vb.CS, KC], [1, 30]]), 0.0),
                             writes=R2ALL)
                    else:
                        S.op("act", lambda h: h.activation(out=V(R2b, 0, [[Vb.CS, KC], [1, 30]]),
                                                           in_=V(CST, j * KC * 30, [[30, KC], [1, 30]]),
                                                           func=AF.Copy),
                             reads=[("CST", j)], writes=R2ALL)
                    prenorm(seg, X, Hh, l, R_MIXPRE)
                    for b in range(4):
                        wu, wuk = ws.get(2 * b)
                        wg, wgk = ws.get(2 * b + 1)
                        for (t0, nt) in seg.tiles:
                            n = ns * nt
                            for cc in range(2):
                                c = 2 * b + cc
                                pu, puk = next_ps()
                                pg, pgk = next_ps()
                                hk = sum([Hh.keys(k, t0, nt) for k in range(KC)], [])

                                def mmf(h, w, ps, cc=cc, t0=t0, nt=nt):
                                    for k in range(KC):
                                        ins = h.matmul(flat(ps, seg, nt), V(w, k * 256 + cc * 128, [[1, 128]]),
                                                       Hh.ap(k, t0, nt), start=(k == 0), stop=(k == KC - 1))
                                    return ins
                                S.op("pe", lambda h: mmf(h, wu, pu), reads=hk + [wuk], writes=[puk])
                                S.op("pe", lambda h: mmf(h, wg, pg), reads=hk + [wgk], writes=[pgk])
                                sg, sgk = rot(GS, "GS")
                                S.op("act", lambda h: h.activation(out=flat(sg, seg, nt), in_=flat(pg, seg, nt),
                                                                   func=AF.Sigmoid, bias=P(l, c, R_BG), scale=1.0),
                                     reads=[pgk, ("PRM", l)], writes=[sgk])
                                S.op("dve", lambda h: h.scalar_tensor_tensor(
                                    out=Vb.ap(c, t0, nt), in0=flat(pu, seg, nt), scalar=P(l, c, R_BU),
                                    in1=flat(sg, seg, nt), op0=ALU.add, op1=ALU.mult),
                                    reads=[puk, sgk, ("PRM", l)], writes=Vb.keys(c, t0, nt))
                                if isS:
                                    S.op("dve", lambda h: h.scalar_tensor_tensor(
                                        out=V(VFT, c * 128, [[8, 16], [1, 8]]), in0=flat(pu, seg, nt),
                                        scalar=P(l, c, R_BU), in1=flat(sg, seg, nt), op0=ALU.add, op1=ALU.mult),
                                        reads=[puk, sgk, ("PRM", l)], writes=[("VF", c)])
                                elif seg.last and t0 + nt == L:
                                    S.op("dve", lambda h: h.scalar_tensor_tensor(
                                        out=V(VFT, c * 128, [[1, 30]]), in0=V(pu, nt - 30, [[1, 30]]),
                                        scalar=P(l, c, R_BU), in1=V(sg, nt - 30, [[1, 30]]), op0=ALU.add, op1=ALU.mult),
                                        reads=[puk, sgk, ("PRM", l)], writes=[("VF", c)])
                    if isS:
                        store_state_S(None, 30, sconv[j], ncs[j], contig=lambda c: V(VFT, c * 128, [[1, 128]]),
                                      ckeys=lambda c: [("VF", c)])
                    elif seg.last:
                        store_T(lambda c: V(VFT, c * 128, [[1, 30]]), lambda c: [("VF", c)], 30, KC,
                                [(ncp[j], 0, 30)])
                    else:
                        S.op("act", lambda h: h.activation(out=V(CST, j * KC * 30, [[30, KC], [1, 30]]),
                                                           in_=V(R2b, Vb.halo + L - 30, [[Vb.CS, KC], [1, 30]]),
                                                           func=AF.Copy),
                             reads=sum([Vb.keys(c, L - 30, 30) for c in range(KC)], []), writes=[("CST", j)])
                    for c in range(KC):
                        dg, dgk = rot(DG, "DG")
                        S.op("dve" if c % 2 == 0 else "pool", lambda h, dg=dg, c=c: h.tensor_tensor(
                            out=V(dg, 0, [[128, 31], [1, 128]]), in0=V(IDN, 0, [[0, 31], [1, 128]]),
                            in1=V(PRM, (l * KC + c) * PR + R_WDW, [[1, 31], [0, 128]]), op=ALU.mult),
                            reads=[("IDN",), ("PRM", l)], writes=[dgk])
                        for (t0, nt) in seg.tiles:
                            pc, pck = next_ps()

                            def mmc(h, dg=dg, pc=pc, c=c, t0=t0, nt=nt):
                                for k in range(31):
                                    ins = h.matmul(flat(pc, seg, nt), V(dg, k * 128, [[1, 128]]),
                                                   Vb.ap(c, t0 - 30 + k, nt), start=(k == 0), stop=(k == 30))
                                return ins
                            S.op("pe", mmc, reads=Vb.keys(c, t0 - 30, nt + 30) + [dgk], writes=[pck])
                            S.op("act", lambda h, pc=pc, c=c, t0=t0, nt=nt: h.activation(
                                out=Cb.ap(c, t0, nt), in_=flat(pc, seg, nt), func=AF.Identity,
                                bias=P(l, c, R_BDW), scale=1.0),
                                reads=[pck, ("PRM", l)], writes=Cb.keys(c, t0, nt))
                    ln_stats = []
                    for (t0, nt) in seg.tiles:
                        n = ns * nt
                        p1, p1k = next_ps()
                        p2, p2k = next_ps()
                        for c in range(KC):
                            cb, cbk = rot(CB16, "CB16")
                            sq, sqk = rot(SQ, "SQ")
                            S.op("dve", lambda h, c=c, cb=cb: h.tensor_copy(out=flat(cb, seg, nt), in_=Cb.ap(c, t0, nt)),
                                 reads=Cb.keys(c, t0, nt), writes=[cbk])
                            S.op("act", lambda h, c=c, sq=sq: h.activation(out=flat(sq, seg, nt), in_=Cb.ap(c, t0, nt),
                                                                           func=AF.Square),
                                 reads=Cb.keys(c, t0, nt), writes=[sqk])
                            S.op("pe", lambda h, c=c, cb=cb: h.matmul(V(p1, 0, [[1, n]]), ONES[:], V(cb, 0, [[1, n]]),
                                                                      start=(c == 0), stop=(c == KC - 1)),
                                 reads=[cbk, ("ONES",)], writes=[p1k])
                            S.op("pe", lambda h, c=c, sq=sq: h.matmul(V(p2, 0, [[1, n]]), ONES[:], V(sq, 0, [[1, n]]),
                                                                      start=(c == 0), stop=(c == KC - 1)),
                                 reads=[sqk, ("ONES",)], writes=[p2k])
                        mu, muk = rot(VS, "VS")
                        rs, rsk = rot(RS, "RS")
                        tt, ttk = rot(TT, "TT")
                        S.op("dve", lambda h: h.tensor_scalar(out=V(mu, 0, [[1, n]]), in0=V(p1, 0, [[1, n]]),
                                                              scalar1=1.0 / D, scalar2=None, op0=ALU.mult),
                             reads=[p1k], writes=[muk])
                        S.op("dve", lambda h: h.tensor_tensor(out=V(tt, 0, [[1, n]]), in0=V(mu, 0, [[1, n]]),
                                                              in1=V(mu, 0, [[1, n]]), op=ALU.mult),
                             reads=[muk], writes=[ttk])
                        S.op("dve", lambda h: h.scalar_tensor_tensor(
                            out=V(rs, 0, [[1, n]]), in0=V(p2, 0, [[1, n]]), scalar=1.0 / D, in1=V(tt, 0, [[1, n]]),
                            op0=ALU.mult, op1=ALU.subtract), reads=[p2k, ttk], writes=[rsk])
                        S.op("act", lambda h: h.activation(out=V(rs, 0, [[1, n]]), in_=V(rs, 0, [[1, n]]),
                                                           func=AF.Ln, bias=EPS, scale=1.0),
                             reads=[rsk], writes=[rsk])
                        S.op("act", lambda h: h.activation(out=V(rs, 0, [[1, n]]), in_=V(rs, 0, [[1, n]]),
                                                           func=AF.Exp, scale=-0.5),
                             reads=[rsk], writes=[rsk])
                        ln_stats.append((mu, muk, rs, rsk))
                    for (t0, nt), (mu, muk, rs, rsk) in zip(seg.tiles, ln_stats):
                        for c in range(KC):
                            S.op("pool" if c >= 4 else "dve", lambda h, c=c: h.tensor_tensor(out=Cb.ap(c, t0, nt), in0=Cb.ap(c, t0, nt),
                                                                        in1=flat(mu, seg, nt), op=ALU.subtract),
                                 reads=Cb.keys(c, t0, nt) + [muk], writes=Cb.keys(c, t0, nt))
                            S.op("dve", lambda h, c=c: h.tensor_tensor(out=Cb.ap(c, t0, nt), in0=Cb.ap(c, t0, nt),
                                                                       in1=flat(rs, seg, nt), op=ALU.mult),
                                 reads=Cb.keys(c, t0, nt) + [rsk], writes=Cb.keys(c, t0, nt))
                            S.op("act", lambda h, c=c: h.activation(out=Hh.ap(c, t0, nt), in_=Cb.ap(c, t0, nt),
                                                                    func=AF.Silu, bias=P(l, c, R_LNB),
                                                                    scale=P(l, c, R_LNG)),
                                 reads=Cb.keys(c, t0, nt) + [("PRM", l)], writes=Hh.keys(c, t0, nt))
                    Mb = Cb
                    for (t0, nt) in seg.tiles:
                        pss = []
                        for b in range(4):
                            wo, wok = ws.get(8 + b)
                            for cc in range(2):
                                c = 2 * b + cc
                                pm, pmk = next_ps()
                                hk = sum([Hh.keys(k, t0, nt) for k in range(KC)], [])

                                def mmo(h, w=wo, ps=pm, cc=cc):
                                    for k in range(KC):
                                        ins = h.matmul(flat(ps, seg, nt), V(w, k * 256 + cc * 128, [[1, 128]]),
                                                       Hh.ap(k, t0, nt), start=(k == 0), stop=(k == KC - 1))
                                    return ins
                                S.op("pe", mmo, reads=hk + [wok], writes=[pmk])
                                S.op("act", lambda h, c=c, pm=pm: h.activation(
                                    out=Mb.ap(c, t0, nt), in_=flat(pm, seg, nt), func=AF.Identity,
                                    bias=P(l, c, R_BOUT), scale=1.0),
                                    reads=[pmk, ("PRM", l)], writes=Mb.keys(c, t0, nt))
                        rs, rsk = sumsq_rstd(seg, nt, lambda c: Mb.ap(c, t0, nt), lambda c: Mb.keys(c, t0, nt))
                        postnorm_add(seg, X, Mb, l, R_MIXPOST, t0, nt, rs, rsk)
                else:
                    units = [(8, b_w_grp[j][g], 2, ("wgrp", j, g)) for g in range(4)]
                    ws = WStream(units, 3)
                    ws.get(0)
                    if isS:
                        pass
                    elif seg.first:
                        S.op("pool", lambda h: h.memset(V(R2, 0, [[Hf.CS, KC], [1, 15]]), 0.0),
                             writes=R2ALL)
                    else:
                        S.op("act", lambda h: h.activation(out=V(R2, 0, [[Hf.CS, KC], [1, 15]]),
                                                           in_=V(PST, j * KC * 15, [[15, KC], [1, 15]]),
                                                           func=AF.Copy),
                             reads=[("PST", j)], writes=R2ALL)
                    prenorm(seg, X, Hf, l, R_MIXPRE)
                    if isS:
                        store_state_S(Hf, 15, spool[j], nps[j])
                    elif seg.last:
                        store_T(lambda c: Hf.ap(c, L - 15, 15), lambda c: Hf.keys(c, L - 15, 15), 15, KC,
                                [(npp[j], 0, 15)])
                    else:
                        S.op("act", lambda h: h.activation(out=V(PST, j * KC * 15, [[15, KC], [1, 15]]),
                                                           in_=V(R2, Hf.halo + L - 15, [[Hf.CS, KC], [1, 15]]),
                                                           func=AF.Copy),
                             reads=sum([Hf.keys(c, L - 15, 15) for c in range(KC)], []), writes=[("PST", j)])
                    WSZ = ns * (15 + L)
                    for c in range(KC):
                        g = c // 2
                        w = POOLW[g]
                        par = c % 2
                        if par == 0:
                            wb = [Buf(R2, "PWa0", seg, 15, base=KC * Hf.CS),
                                  Buf(R2, "PWb0", seg, 15, base=KC * Hf.CS + WSZ)]
                        else:
                            wb = [Buf(R3, "PWa1", seg, 15, base=0), Buf(R3, "PWb1", seg, 15, base=WSZ)]
                        allk = lambda bf: [(bf.name, 0, ti) for ti in range(len(seg.tiles))] + [(bf.name, 0, "h")]
                        src, srck = Hf, [("Hf", c, ti) for ti in range(len(seg.tiles))] + [("Hf", c, "h")]
                        srcc = c
                        lo = -15
                        sh = 1
                        step = 0
                        while sh < w:
                            dstb = wb[step % 2]
                            lo2 = lo + sh
                            nn = L - lo2
                            S.op("pool" if par else "dve", lambda h, src=src, srcc=srcc, dstb=dstb, lo2=lo2, nn=nn, sh=sh: h.tensor_tensor(
                                out=dstb.ap(0, lo2, nn), in0=src.ap(srcc, lo2, nn), in1=src.ap(srcc, lo2 - sh, nn),
                                op=ALU.add), reads=srck, writes=allk(dstb))
                            src, srck, srcc = dstb, allk(dstb), 0
                            lo = lo2
                            sh *= 2
                            step += 1
                        hkeys = sum([Hh.keys(c, t0, nt) for (t0, nt) in seg.tiles], [])
                        S.op("dve", lambda h, src=src, c=c, w=w: h.scalar_tensor_tensor(
                            out=Hh.ap(c, 0, L), in0=src.ap(0, 0, L), scalar=1.0 / w, in1=Hf.ap(c, 0, L),
                            op0=ALU.mult, op1=ALU.subtract),
                            reads=srck + [("Hf", c, ti) for ti in range(len(seg.tiles))], writes=hkeys)
                        if seg.first:
                            tt, ttk = rot(TT, "TT")
                            S.op("dve", lambda h, src=src, g=g, tt=tt: h.tensor_tensor(
                                out=V(tt, 0, [[1, 16]]), in0=src.ap(0, 0, 16), in1=IC[:, 16 * g:16 * g + 16],
                                op=ALU.mult), reads=srck + [("IC",)], writes=[ttk])
                            S.op("dve", lambda h, c=c, tt=tt: h.tensor_tensor(
                                out=Hh.ap(c, 0, 16), in0=V(tt, 0, [[1, 16]]), in1=Hf.ap(c, 0, 16),
                                op=ALU.subtract), reads=[ttk, ("Hf", c, 0)], writes=Hh.keys(c, 0, 16))
                    Mb = Cb
                    for (t0, nt) in seg.tiles:
                        for g in range(4):
                            wgp, wgpk = ws.get(g)
                            for cc in range(2):
                                c = 2 * g + cc
                                pm, pmk = next_ps()
                                hk = Hh.keys(2 * g, t0, nt) + Hh.keys(2 * g + 1, t0, nt)

                                def mmg(h, w=wgp, ps=pm, cc=cc, g=g):
                                    for k in range(2):
                                        ins = h.matmul(flat(ps, seg, nt), V(w, k * 256 + cc * 128, [[1, 128]]),
                                                       Hh.ap(2 * g + k, t0, nt), start=(k == 0), stop=(k == 1))
                                    return ins
                                S.op("pe", mmg, reads=hk + [wgpk], writes=[pmk])
                                S.op("act", lambda h, c=c, pm=pm: h.activation(
                                    out=Mb.ap(c, t0, nt), in_=flat(pm, seg, nt), func=AF.Identity,
                                    scale=P(l, c, R_SCALE)),
                                    reads=[pmk, ("PRM", l)], writes=Mb.keys(c, t0, nt) + PW1ALL)
                        rs, rsk = sumsq_rstd(seg, nt, lambda c: Mb.ap(c, t0, nt), lambda c: Mb.keys(c, t0, nt))
                        postnorm_add(seg, X, Mb, l, R_MIXPOST, t0, nt, rs, rsk)

                units = []
                for b in range(11):
                    units.append((8, f_w_up[l][:, 256 * b:256 * b + 256], KC, ("wup", l, b, 0)))
                    units.append((8, f_w_up[l][:, DFF + 256 * b:DFF + 256 * b + 256], KC, ("wup", l, b, 1)))
                for b in range(8):
                    units.append((22, f_w_down[l][:, 128 * b:128 * b + 128], JC, ("wdn", l, b)))
                ws = WStream(units, 3)
                ws.get(0)
                if isS:
                    pass
                elif seg.first:
                    S.op("pool", lambda h: h.memset(V(R1b, 0, [[Hh.CS, KC], [1, 2]]), 0.0),
                         writes=[("H", c, "h") for c in range(KC)])
                else:
                    S.op("act", lambda h: h.activation(out=V(R1b, 0, [[Hh.CS, KC], [1, 2]]),
                                                       in_=V(FH, l * KC * 2, [[2, KC], [1, 2]]), func=AF.Copy),
                         reads=[("FH", l)], writes=[("H", c, "h") for c in range(KC)])
                prenorm(seg, X, Hh, l, R_FFNPRE)
                if (not isS) and (not seg.last):
                    S.op("act", lambda h: h.activation(out=V(FH, l * KC * 2, [[2, KC], [1, 2]]),
                                                       in_=V(R1b, Hh.halo + L - 2, [[Hh.CS, KC], [1, 2]]),
                                                       func=AF.Copy),
                         reads=sum([Hh.keys(c, L - 2, 2) for c in range(KC)], []), writes=[("FH", l)])
                for b in range(11):
                    wgt, wgtk = ws.get(2 * b)
                    wvl, wvlk = ws.get(2 * b + 1)
                    for ti, (t0, nt) in enumerate(seg.tiles):
                        ntp = nt + 2
                        lastt = (ti == len(seg.tiles) - 1)
                        for cc in range(2):
                            jj = 2 * b + cc
                            res = []
                            for which, (w, wk) in enumerate(((wgt, wgtk), (wvl, wvlk))):
                                uch = jj + which * JC
                                ps, pk = next_ps()
                                hk = sum([Hh.keys(k, t0 - 2, ntp) for k in range(KC)], [])
                                if isS:
                                    pfull = V(ps, 0, [[ntp, ns], [1, ntp]])
                                    pmain = V(ps, 2, [[ntp, ns], [1, nt]])
                                    phalo = V(ps, 0, [[ntp, ns], [1, 2]])

                                    def mmu(h, w=w, cc=cc, pmain=pmain, phalo=phalo, uch=uch):
                                        for k in range(KC):
                                            h.matmul(pmain, V(w, k * 256 + cc * 128, [[1, 128]]),
                                                     Hh.ap(k, t0, nt), start=(k == 0), stop=(k == KC - 1))
                                        return h.matmul(phalo, IDN[:], V(UST, uch * 32, [[2, ns], [1, 2]]),
                                                        start=True, stop=True)
                                    S.op("pe", mmu, reads=hk + [wk, ("UST",), ("IDN",)], writes=[pk])
                                else:
                                    pfull = V(ps, 0, [[1, ntp]])

                                    def mmu(h, w=w, cc=cc, pfull=pfull):
                                        for k in range(KC):
                                            ins = h.matmul(pfull, V(w, k * 256 + cc * 128, [[1, 128]]),
                                                           Hh.ap(k, t0 - 2, ntp), start=(k == 0), stop=(k == KC - 1))
                                        return ins
                                    S.op("pe", mmu, reads=hk + [wk], writes=[pk])

                                def pv(sh, ps=ps):
                                    if isS:
                                        return V(ps, sh, [[ntp, ns], [1, nt]])
                                    return V(ps, sh, [[1, nt]])
                                acc, acck = rot(GS if which == 0 else VS, "GS" if which == 0 else "VS")
                                if isS and which == 1:
                                    S.op("dve", lambda h, acc=acc, pv=pv, uch=uch: h.tensor_scalar(
                                        out=flat(acc, seg, nt), in0=pv(2), scalar1=FP(l, uch, 2), scalar2=FP(l, uch, 3),
                                        op0=ALU.mult, op1=ALU.add),
                                        reads=[pk, ("FPRM", l)], writes=[acck])
                                else:
                                    S.op("act", lambda h, acc=acc, pv=pv, uch=uch: h.activation(
                                        out=flat(acc, seg, nt), in_=pv(2), func=AF.Identity,
                                        bias=FP(l, uch, 3), scale=FP(l, uch, 2)),
                                        reads=[pk, ("FPRM", l)], writes=[acck])
                                for tap in (1, 0):
                                    S.op("dve", lambda h, acc=acc, pv=pv, uch=uch, tap=tap: h.scalar_tensor_tensor(
                                        out=flat(acc, seg, nt), in0=pv(tap), scalar=FP(l, uch, tap),
                                        in1=flat(acc, seg, nt), op0=ALU.mult, op1=ALU.add),
                                        reads=[pk, acck, ("FPRM", l)], writes=[acck])
                                if lastt and seg.last:
                                    src = V(ps, nt, [[ntp, ns], [1, 2]]) if isS else V(ps, nt, [[1, 2]])
                                    dst = V(UO, uch * 32, [[2, ns], [1, 2]]) if isS else V(UO, uch * 32, [[1, 2]])
                                    S.op("act", lambda h, src=src, dst=dst: h.activation(out=dst, in_=src, func=AF.Copy),
                                         reads=[pk], writes=[("UO", uch)])
                                res.append((acc, acck))
                            (ga, gak), (va, vak) = res
                            S.op("act", lambda h, ga=ga: h.activation(out=flat(ga, seg, nt), in_=flat(ga, seg, nt),
                                                                      func=AF.Silu), reads=[gak], writes=[gak])
                            S.op("pool", lambda h, ga=ga, va=va, jj=jj: h.tensor_tensor(
                                out=A.ap(jj, t0, nt), in0=flat(ga, seg, nt), in1=flat(va, seg, nt), op=ALU.mult),
                                reads=[gak, vak], writes=A.keys(jj, t0, nt))
                if seg.last:
                    R = 32 if isS else 2
                    dstT = nfs if isS else nfp
                    for q in range(6):
                        c0 = q * 8
                        n = min(8, UC - c0)
                        store_T(lambda c, c0=c0, R=R: V(UO, (c0 + c) * 32, [[1, R]]),
                                lambda c, c0=c0: [("UO", c0 + c)], R, n,
                                [(dstT[l][:, c0 * 128:(c0 + n) * 128], 0, R)])
                Fb = Cb
                for b in range(8):
                    wd, wdk = ws.get(22 + b)
                    for (t0, nt) in seg.tiles:
                        m = b
                        pf, pfk = next_ps()
                        ak = sum([A.keys(q, t0, nt) for q in range(JC)], [])

                        def mmd(h, w=wd, ps=pf, t0=t0, nt=nt):
                            for q in range(JC):
                                ins = h.matmul(flat(ps, seg, nt), V(w, q * 128, [[1, 128]]),
                                               A.ap(q, t0, nt), start=(q == 0), stop=(q == JC - 1))
                            return ins
                        S.op("pe", mmd, reads=ak + [wdk], writes=[pfk])
                        S.op("act", lambda h, m=m, pf=pf, t0=t0, nt=nt: h.activation(
                            out=Fb.ap(m, t0, nt), in_=flat(pf, seg, nt), func=AF.Copy),
                            reads=[pfk], writes=Fb.keys(m, t0, nt))
                if isS and l + 1 < NL:
                    load_halo_S(l + 1)
                for (t0, nt) in seg.tiles:
                    rs, rsk = sumsq_rstd(seg, nt, lambda c: Fb.ap(c, t0, nt), lambda c: Fb.keys(c, t0, nt))
                    postnorm_add(seg, X, Fb, l, R_FFNPOST, t0, nt, rs, rsk)

            Yb = Cb
            for (t0, nt) in seg.tiles:
                rs, rsk = sumsq_rstd(seg, nt, lambda c: X.ap(c, t0, nt), lambda c: X.keys(c, t0, nt))
                for c in range(KC):
                    S.op("dve", lambda h, c=c: h.scalar_tensor_tensor(
                        out=Yb.ap(c, t0, nt), in0=X.ap(c, t0, nt), scalar=P(0, c, R_FINAL),
                        in1=flat(rs, seg, nt), op0=ALU.mult, op1=ALU.mult),
                        reads=X.keys(c, t0, nt) + [rsk, ("PRM", 0)], writes=Yb.keys(c, t0, nt))
            if isS:
                store_T(lambda c: V(R3, c * Yb.CS, [[1, 128]]), lambda c: [("C", c, 0)], 128, KC,
                        [(ys[:, :], 0, 128)])
            else:
                p0 = seg.idx * LP
                r = 0
                while r < LP:
                    nr = min(128, LP - r)
                    a = p0 + r
                    pieces = []
                    if a + nr > NMETA:
                        skip = max(0, NMETA - a)
                        pieces.append((yp[a + skip - NMETA:a + nr - NMETA, :], skip, nr - skip))
                    if pieces:
                        store_T(lambda c, r=r, nr=nr: Yb.ap(c, r, nr), lambda c, r=r, nr=nr: Yb.keys(c, r, nr),
                                nr, KC, pieces)
                    r += nr
        S.finish()
    return nc


def _prep(inputs):
    f = lambda k: np.ascontiguousarray(np.asarray(inputs[k], dtype=np.float32))
    prm = np.zeros((NL, PR, D), np.float32)
    for l in range(NL):
        j = l // 2
        prm[l, R_MIXPRE] = f("mix_pre")[l]
        prm[l, R_MIXPOST] = f("mix_post")[l]
        prm[l, R_FFNPRE] = f("ffn_pre")[l]
        prm[l, R_FFNPOST] = f("ffn_post")[l]
        prm[l, R_FINAL] = f("final_norm")
        if l % 2 == 0:
            prm[l, R_BU] = f("a_b_in")[j, :D]
            prm[l, R_BG] = f("a_b_in")[j, D:]
            prm[l, R_BDW] = f("a_b_dw")[j]
            prm[l, R_LNG] = f("a_ln_g")[j]
            prm[l, R_LNB] = f("a_ln_b")[j]
            prm[l, R_BOUT] = f("a_b_out")[j]
            prm[l, R_WDW:R_WDW + 31] = f("a_w_dw")[j]
        else:
            prm[l, R_SCALE] = f("b_scale")[j]
    fprm = np.concatenate([f("f_w_dw"), f("f_b_dw")[:, None, :]], axis=1)
    shared = dict(meta=f("meta_tokens"), prm=prm, fprm=np.ascontiguousarray(fprm),
                  a_w_in=f("a_w_in"), a_w_out=f("a_w_out"), b_w_grp=f("b_w_grp"),
                  f_w_up=f("f_w_up"), f_w_down=f("f_w_down"))
    xpr, xsa = f("x_prompt"), f("x_sample")
    sc, sp_, sf = f("state_conv"), f("state_pool"), f("state_ffn")
    in_maps = []
    for c in range(8):
        m = dict(shared)
        m["xp"] = xpr[c]
        m["xs"] = np.ascontiguousarray(xsa[16 * c:16 * c + 16].reshape(128, D))
        m["sconv"] = np.ascontiguousarray(sc[:, 16 * c:16 * c + 16].reshape(2, 480, D))
        m["spool"] = np.ascontiguousarray(sp_[:, 16 * c:16 * c + 16].reshape(2, 240, D))
        m["sffn"] = np.ascontiguousarray(sf[:, 16 * c:16 * c + 16].reshape(NL, 32, 2 * DFF))
        in_maps.append(m)
    return in_maps


_NC_CACHE = {}


def kernel(**inputs):
    in_maps = _prep(inputs)
    if "nc" not in _NC_CACHE:
        _NC_CACHE["nc"] = build()
    nc = _NC_CACHE["nc"]
    res = run_bass_kernel_spmd(nc, in_maps, core_ids=list(range(8)))
    r = res.results
    y_prompt = np.stack([r[c]["yp"] for c in range(8)], 0)
    y_sample = np.concatenate([r[c]["ys"].reshape(16, 8, D) for c in range(8)], 0)
    ncp = np.stack([r[c]["ncp"] for c in range(8)], 1)
    npp = np.stack([r[c]["npp"] for c in range(8)], 1)
    nfp = np.stack([r[c]["nfp"] for c in range(8)], 1)
    ncs = np.concatenate([r[c]["ncs"].reshape(2, 16, 30, D) for c in range(8)], 1)
    nps = np.concatenate([r[c]["nps"].reshape(2, 16, 15, D) for c in range(8)], 1)
    nfs = np.concatenate([r[c]["nfs"].reshape(NL, 16, 2, 2 * DFF) for c in range(8)], 1)
    return (y_prompt.astype(np.float32), y_sample.astype(np.float32), ncp.astype(np.float32),
            npp.astype(np.float32), nfp.astype(np.float32), ncs.astype(np.float32),
            nps.astype(np.float32), nfs.astype(np.float32))
```
